# Optimizing a Trainium2 kernel written in Bass

```python
import math
import jax, jax.numpy as jnp
from jax import lax
import numpy as np

D_MODEL = 1024
BATCH = 1
SEQ = 16384
DEPTH = 4
DEC_BATCH = 4
DEC_SEQ = 4096
PAST_LEN = 128

D_HYENA = 512
HYENA_GROUPS = 8
D_HGRN = D_MODEL - D_HYENA
HGRN_HEADS = 4
HGRN_DK = D_HGRN // HGRN_HEADS
HGRN_DV = D_HGRN // HGRN_HEADS
D_IN = 3 * D_HYENA + 5 * D_HGRN
D_FF = 4 * D_MODEL
CHUNK = 64
FILTER_EMB = 33
FILTER_BANDS = (FILTER_EMB - 1) // 2
FILTER_HIDDEN = 64
DECAY_TARGET = 1e-2
FAST_DECAY_PCT = 0.3
SLOW_DECAY_PCT = 1.5
EPS = 1e-6

kernel_name = "hyena_hgrn2_parallel_encoder"

F32 = jnp.float32


def _rmsnorm(x, gain):
    xf = x.astype(F32)
    y = xf * lax.rsqrt(jnp.mean(xf * xf, axis=-1, keepdims=True) + EPS)
    return (y * gain.astype(F32)).astype(x.dtype)


def _group_rmsnorm(x, gain, n_groups):
    b, l, c = x.shape
    y = _rmsnorm(x.reshape(b, l, n_groups, c // n_groups), gain.reshape(n_groups, c // n_groups))
    return y.reshape(b, l, c)


def _dwconv3(x, w, b):
    L = x.shape[1]
    xp = jnp.pad(x, ((0, 0), (1, 1), (0, 0)))
    return xp[:, :L] * w[0] + xp[:, 1:L + 1] * w[1] + xp[:, 2:] * w[2] + b


def _hyena_filter(L, w1, b1, w2, b2, w3, b3, w4, freq):
    t = jnp.linspace(0.0, 1.0, L, dtype=F32)[:, None]
    ang = (2.0 * math.pi / L) * jnp.arange(L, dtype=F32)[:, None]
    bands = jnp.linspace(1e-4, FILTER_BANDS - 1, FILTER_BANDS, dtype=F32)[None, :]
    feats = jnp.concatenate([t, jnp.cos(bands * ang), -jnp.sin(bands * ang)], axis=-1)
    fr = freq.astype(F32)
    h = jnp.sin(fr * (feats @ w1.astype(F32) + b1.astype(F32)))
    h = jnp.sin(fr * (h @ w2.astype(F32) + b2.astype(F32)))
    h = jnp.sin(fr * (h @ w3.astype(F32) + b3.astype(F32)))
    h = h @ w4.astype(F32)
    deltas = jnp.abs(jnp.linspace(math.log(DECAY_TARGET) / SLOW_DECAY_PCT,
                                  math.log(DECAY_TARGET) / FAST_DECAY_PCT, D_HYENA, dtype=F32))
    window = jnp.exp(-t * deltas)
    h = h.reshape(L, 2, D_HYENA) * window[:, None, :]
    h_fwd, h_bwd = h[:, 0], h[:, 1]
    return jnp.concatenate([h_fwd, jnp.zeros((1, D_HYENA), F32), h_bwd[:0:-1]], axis=0)


def _hyena(u, kernel, skip):
    L = u.shape[1]
    uf = u.astype(F32)
    x0, x1, v = jnp.split(uf, 3, axis=-1)
    z = x1 * v
    Z = jnp.fft.rfft(z, n=2 * L, axis=1)
    K = jnp.fft.rfft(kernel, axis=0)
    y = jnp.fft.irfft(Z * K[None], n=2 * L, axis=1)[:, :L] + skip.astype(F32) * z
    return (x0 * y).astype(u.dtype)


def _chunk_scan(q, k, logf, v):
    B, L, H, dk = q.shape
    dv = v.shape[-1]
    n = L // CHUNK

    def to_chunks(a):
        return a.reshape(B, n, CHUNK, H, a.shape[-1]).transpose(1, 0, 3, 2, 4)

    causal = jnp.tril(jnp.ones((CHUNK, CHUNK), dtype=bool))[:, :, None]

    def step(S, xs):
        qc, kc, gc, vc = xs
        b = jnp.cumsum(gc, axis=2)
        o_inter = jnp.einsum('bhtd,bhde->bhte', qc * jnp.exp(b), S)
        rel = b[:, :, :, None, :] - b[:, :, None, :, :]
        decay = jnp.exp(jnp.where(causal, rel, -jnp.inf))
        scores = jnp.einsum('bhtd,bhsd,bhtsd->bhts', qc, kc, decay)
        o_intra = jnp.einsum('bhts,bhse->bhte', scores, vc)
        b_last = b[:, :, -1:, :]
        S_new = jnp.exp(b_last[:, :, 0, :, None]) * S + jnp.einsum(
            'bhsd,bhse->bhde', kc * jnp.exp(b_last - b), vc)
        return S_new, o_inter + o_intra

    S0 = jnp.zeros((B, H, dk, dv), F32)
    _, o = lax.scan(step, S0, (to_chunks(q), to_chunks(k), to_chunks(logf), to_chunks(v)))
    return o.transpose(1, 0, 3, 2, 4).reshape(B, L, H, dv)


def _hgrn2(hg, lb_fwd, lb_bwd, out_gain):
    B, L, _ = hg.shape
    q, f_fwd, f_bwd, i, g = jnp.split(hg.astype(F32), 5, axis=-1)
    q = jax.nn.silu(q)

    def heads(a):
        return a.reshape(B, L, HGRN_HEADS, a.shape[-1] // HGRN_HEADS)

    def log_forget(z, lb):
        lb = lb.astype(F32)
        return jnp.logaddexp(jnp.log(lb), jnp.log1p(-lb) + jax.nn.log_sigmoid(z))

    def flip(a):
        return jnp.flip(a, axis=1)

    gf = heads(log_forget(f_fwd, lb_fwd))
    gb = heads(log_forget(f_bwd, lb_bwd))
    qh, ih = heads(q), heads(i)
    o_f = _chunk_scan(qh, -jnp.expm1(gf), gf, ih)
    o_b = flip(_chunk_scan(flip(qh), flip(-jnp.expm1(gb)), flip(gb), flip(ih)))
    o = _rmsnorm(o_f + o_b, out_gain.reshape(HGRN_HEADS, HGRN_DV))
    o = o.reshape(B, L, D_HGRN) * jax.nn.silu(g)
    return o.astype(hg.dtype)


def _lower_bounds(table):
    c = jnp.cumsum(jax.nn.softmax(table.astype(F32), axis=1), axis=1)
    return c - c[:, :1]


def _trunk(x, lbs, norm_mix_pre, norm_mix_post, norm_ffn_pre, norm_ffn_post, w_in,
           hyena_conv_w, hyena_conv_b, filt_w1, filt_b1, filt_w2, filt_b2, filt_w3, filt_b3,
           filt_w4, filt_freq, hyena_skip, hyena_out_norm, hgrn_out_norm, w_out,
           ffn_w_up, ffn_conv_w, ffn_conv_b, ffn_w_down):
    L = x.shape[1]
    for l in range(DEPTH):
        h = _rmsnorm(x, norm_mix_pre[l])
        proj = h @ w_in[l]
        hy = _dwconv3(proj[..., :3 * D_HYENA], hyena_conv_w[l], hyena_conv_b[l])
        hg = proj[..., 3 * D_HYENA:]
        kernel = _hyena_filter(L, filt_w1[l], filt_b1[l], filt_w2[l], filt_b2[l], filt_w3[l],
                               filt_b3[l], filt_w4[l], filt_freq[l])
        y_hy = _group_rmsnorm(_hyena(hy, kernel, hyena_skip[l]), hyena_out_norm[l], HYENA_GROUPS)
        y_hg = _hgrn2(hg, lbs[0, l], lbs[1, l], hgrn_out_norm[l])
        mix = jnp.concatenate([y_hy, y_hg], axis=-1) @ w_out[l]
        x = x + _rmsnorm(mix, norm_mix_post[l])
        h = _rmsnorm(x, norm_ffn_pre[l])
        u = _dwconv3(h @ ffn_w_up[l], ffn_conv_w[l], ffn_conv_b[l])
        a, b = jnp.split(u, 2, axis=-1)
        ff = (jax.nn.gelu(a, approximate=True) * b) @ ffn_w_down[l]
        x = x + _rmsnorm(ff, norm_ffn_post[l])
    return x


def setup_inputs(seed: int = 0) -> dict:
    key = jax.random.key(seed)
    ks = jax.random.split(key, 32)

    def nrm(k, shape, scale):
        return jax.random.normal(k, shape, F32) * scale

    def gain(k, shape):
        return 1.0 + 0.05 * jax.random.normal(k, shape, F32)

    centre = jnp.array([0.0, 1.0, 0.0], F32)[:, None]
    return {
        "x_prompt": nrm(ks[0], (BATCH, SEQ, D_MODEL), 1.0),
        "x_sample": nrm(ks[1], (DEC_BATCH, DEC_SEQ, D_MODEL), 1.0),
        "norm_mix_pre": gain(ks[2], (DEPTH, D_MODEL)),
        "norm_mix_post": gain(ks[3], (DEPTH, D_MODEL)),
        "norm_ffn_pre": gain(ks[4], (DEPTH, D_MODEL)),
        "norm_ffn_post": gain(ks[5], (DEPTH, D_MODEL)),
        "w_in": nrm(ks[6], (DEPTH, D_MODEL, D_IN), D_MODEL ** -0.5),
        "hyena_conv_w": centre + nrm(ks[7], (DEPTH, 3, 3 * D_HYENA), 0.3),
        "hyena_conv_b": nrm(ks[8], (DEPTH, 3 * D_HYENA), 0.01),
        "filt_w1": nrm(ks[9], (DEPTH, FILTER_EMB, FILTER_HIDDEN), FILTER_EMB ** -0.5),
        "filt_b1": nrm(ks[10], (DEPTH, FILTER_HIDDEN), 0.1),
        "filt_w2": nrm(ks[11], (DEPTH, FILTER_HIDDEN, FILTER_HIDDEN), FILTER_HIDDEN ** -0.5),
        "filt_b2": nrm(ks[12], (DEPTH, FILTER_HIDDEN), 0.1),
        "filt_w3": nrm(ks[13], (DEPTH, FILTER_HIDDEN, FILTER_HIDDEN), FILTER_HIDDEN ** -0.5),
        "filt_b3": nrm(ks[14], (DEPTH, FILTER_HIDDEN), 0.1),
        "filt_w4": nrm(ks[15], (DEPTH, FILTER_HIDDEN, 2 * D_HYENA), FILTER_HIDDEN ** -0.5),
        "filt_freq": gain(ks[16], (DEPTH, FILTER_HIDDEN)),
        "hyena_skip": nrm(ks[17], (DEPTH, D_HYENA), 1.0),
        "hyena_out_norm": gain(ks[18], (DEPTH, D_HYENA)),
        "hgrn_lower_bounds": nrm(ks[19], (2, DEPTH, D_HGRN), 0.1),
        "hgrn_out_norm": gain(ks[20], (DEPTH, D_HGRN)),
        "w_out": nrm(ks[21], (DEPTH, D_MODEL, D_MODEL), D_MODEL ** -0.5),
        "ffn_w_up": nrm(ks[22], (DEPTH, D_MODEL, 2 * D_FF), D_MODEL ** -0.5),
        "ffn_conv_w": centre + nrm(ks[23], (DEPTH, 3, 2 * D_FF), 0.3),
        "ffn_conv_b": nrm(ks[24], (DEPTH, 2 * D_FF), 0.01),
        "ffn_w_down": nrm(ks[25], (DEPTH, D_FF, D_MODEL), D_FF ** -0.5),
    }


def reference(x_prompt, x_sample, norm_mix_pre, norm_mix_post, norm_ffn_pre, norm_ffn_post, w_in,
              hyena_conv_w, hyena_conv_b, filt_w1, filt_b1, filt_w2, filt_b2, filt_w3, filt_b3,
              filt_w4, filt_freq, hyena_skip, hyena_out_norm, hgrn_lower_bounds, hgrn_out_norm,
              w_out, ffn_w_up, ffn_conv_w, ffn_conv_b, ffn_w_down):
    lbs = _lower_bounds(hgrn_lower_bounds)
    y_prompt = _trunk(x_prompt, lbs, norm_mix_pre, norm_mix_post, norm_ffn_pre, norm_ffn_post,
                      w_in, hyena_conv_w, hyena_conv_b, filt_w1, filt_b1, filt_w2, filt_b2,
                      filt_w3, filt_b3, filt_w4, filt_freq, hyena_skip, hyena_out_norm,
                      hgrn_out_norm, w_out, ffn_w_up, ffn_conv_w, ffn_conv_b, ffn_w_down)
    y_sample = _trunk(x_sample, lbs, norm_mix_pre, norm_mix_post, norm_ffn_pre, norm_ffn_post,
                      w_in, hyena_conv_w, hyena_conv_b, filt_w1, filt_b1, filt_w2, filt_b2,
                      filt_w3, filt_b3, filt_w4, filt_freq, hyena_skip, hyena_out_norm,
                      hgrn_out_norm, w_out, ffn_w_up, ffn_conv_w, ffn_conv_b, ffn_w_down)
    return (y_prompt, y_sample)
```

```python
import math
import numpy as np
from contextlib import ExitStack
import concourse.bass as bass
import concourse.mybir as mybir

F32 = mybir.dt.float32
BF16 = mybir.dt.bfloat16
AF = mybir.ActivationFunctionType
ALU = mybir.AluOpType
AX = mybir.AxisListType

SAME_ENG_SYNC = {"pool", "dve", "act"}


class Buf:
    __slots__ = ("t", "name", "last_w", "readers")

    def __init__(self, t, name):
        self.t = t
        self.name = name
        self.last_w = None
        self.readers = []

    def __getitem__(self, idx):
        return self.t[idx]

    def ap(self):
        return self.t.ap() if hasattr(self.t, "ap") and callable(getattr(self.t, "ap")) else self.t[:]


class Sched:
    NDSEM = 12

    def __init__(self, nc, es):
        self.nc = nc
        self.es = es
        self.eng = {"pe": nc.tensor, "act": nc.scalar, "dve": nc.vector, "pool": nc.gpsimd, "sp": nc.sync}
        self.prog = {e: [] for e in self.eng}
        self.csem = {e: es.enter_context(nc.semaphore("c_" + e)) for e in ("pe", "act", "dve", "pool")}
        self.ccnt = {e: 0 for e in self.csem}
        self.dsem = {q: [es.enter_context(nc.semaphore(f"d_{q}{i}")) for i in range(self.NDSEM)]
                     for q in ("sp", "pool", "act")}
        self.dcnt = {q: [0] * self.NDSEM for q in self.dsem}
        self.dnext = {q: 0 for q in self.dsem}
        self.seen = {e: {} for e in self.eng}
        self.ninst = 0
        self.final_events = []

    def _sem_of(self, key):
        if key[0] == "c":
            return self.csem[key[1]]
        return self.dsem[key[1]][key[2]]

    def _deps(self, engine, reads, writes):
        deps = {}
        def add(ev):
            if ev is None:
                return
            k, v = ev
            if deps.get(k, 0) < v:
                deps[k] = v
        for b in reads:
            add(b.last_w)
        for b in writes:
            add(b.last_w)
            for ev in b.readers:
                add(ev)
        waits = []
        for k, v in deps.items():
            if k == ("c", engine) and (engine == "pe" or engine not in SAME_ENG_SYNC):
                continue
            if self.seen[engine].get(k, 0) < v:
                self.seen[engine][k] = v
                waits.append((k, v))
        return waits

    def _record(self, ev, reads, writes):
        for b in writes:
            b.last_w = ev
            b.readers = []
        for b in reads:
            b.readers.append(ev)

    def op(self, engine, fn, reads=(), writes=()):
        waits = self._deps(engine, reads, writes)
        self.ccnt[engine] += 1
        ev = (("c", engine), self.ccnt[engine])
        self._record(ev, reads, writes)
        self.prog[engine].append((waits, fn, (self.csem[engine], 1)))
        self.ninst += 1
        return ev

    def dma(self, queue, out, in_, reads=(), writes=(), **kw):
        waits = self._deps(queue, reads, writes)
        i = self.dnext[queue]
        self.dnext[queue] = (i + 1) % self.NDSEM
        prev = self.dcnt[queue][i]
        key = ("d", queue, i)
        if prev > 0 and self.seen[queue].get(key, 0) < prev:
            self.seen[queue][key] = prev
            waits.append((key, prev))
        self.dcnt[queue][i] = prev + 16
        ev = (key, prev + 16)
        self._record(ev, reads, writes)
        fn = lambda e, out=out, in_=in_, kw=kw: e.dma_start(out=out, in_=in_, **kw)
        self.prog[queue].append((waits, fn, (self.dsem[queue][i], 16)))
        self.ninst += 1
        return ev

    def emit(self, final_bufs=()):
        if not hasattr(self, "ecount"):
            self.ecount = {e: 0 for e in self.csem}
            self.emap = {e: {} for e in self.csem}
        needed = set()
        for engine in self.prog:
            for waits, fn, inc in self.prog[engine]:
                for k, v in waits:
                    if k[0] == "c":
                        needed.add((k[1], v))
        logical = {}
        base = {e: self.ccnt[e] - sum(1 for w, f, inc in self.prog[e] if inc[0] is self.csem.get(e)) for e in self.csem}
        for engine in self.csem:
            c = base[engine]
            ids = []
            for waits, fn, inc in self.prog[engine]:
                if inc[0] is self.csem[engine]:
                    c += 1
                    ids.append(c)
                else:
                    ids.append(None)
            logical[engine] = ids
            last = [i for i in ids if i is not None]
            if last:
                needed.add((engine, last[-1]))
            for i in ids:
                if i is not None and (engine, i) in needed:
                    self.ecount[engine] += 1
                    self.emap[engine][i] = self.ecount[engine]
        fin = {}
        for e2 in self.csem:
            if self.ccnt[e2] > 0:
                fin[("c", e2)] = self.ccnt[e2]
        for q in self.dsem:
            for i in range(self.NDSEM):
                if self.dcnt[q][i] > 0:
                    fin[("d", q, i)] = self.dcnt[q][i]

        def semval(k, v):
            if k[0] == "c":
                m = self.emap[k[1]]
                if v in m:
                    return m[v]
                cands = [lv for lv in m if lv >= v]
                return m[min(cands)]
            return v

        with self.nc.Block() as block:
            def mk(engine):
                def body(e):
                    ids = logical.get(engine)
                    for idx, (waits, fn, (sem, n)) in enumerate(self.prog[engine]):
                        for k, v in waits:
                            e.wait_ge(self._sem_of(k), semval(k, v))
                        inst = fn(e)
                        if ids is not None and ids[idx] is not None:
                            if (engine, ids[idx]) in needed:
                                inst.then_inc(sem, n)
                        else:
                            inst.then_inc(sem, n)
                    for k, v in fin.items():
                        if k == ("c", engine):
                            continue
                        if self.seen[engine].get(k, 0) < v:
                            self.seen[engine][k] = v
                            e.wait_ge(self._sem_of(k), semval(k, v))
                return body
            block.sync(mk("sp"))
            block.tensor(mk("pe"))
            block.scalar(mk("act"))
            block.vector(mk("dve"))
            block.gpsimd(mk("pool"))
        self.prog = {e: [] for e in self.eng}


_UNIQ = [0]


def _uname(name):
    _UNIQ[0] += 1
    return f"{name}_{_UNIQ[0]}"


class Pool:
    def __init__(self, nc, es, name, n, shape, dtype, space="sbuf"):
        self.bufs = []
        name = _uname(name)
        for i in range(n):
            if space == "sbuf":
                t = es.enter_context(nc.sbuf_tensor(f"{name}{i}", list(shape), dtype))
            else:
                t = es.enter_context(nc.psum_tensor(f"{name}{i}", list(shape), dtype))
            self.bufs.append(Buf(t, f"{name}{i}"))
        self.i = 0

    def next(self):
        b = self.bufs[self.i]
        self.i = (self.i + 1) % len(self.bufs)
        return b


class ViewPool:
    def __init__(self, nc, es, name, n, w, dtype):
        per = (512 if dtype == F32 else 1024) // w
        self.bufs = []
        nb = (n + per - 1) // per
        for b in range(nb):
            t = es.enter_context(nc.psum_tensor(_uname(name), [128, per * w], dtype))
            for i in range(per):
                if len(self.bufs) < n:
                    self.bufs.append(Buf(t[:, i * w:(i + 1) * w], f"{name}{b}_{i}"))
        self.i = 0

    def next(self):
        b = self.bufs[self.i]
        self.i = (self.i + 1) % len(self.bufs)
        return b


def sb(nc, es, name, shape, dtype):
    name = _uname(name)
    return Buf(es.enter_context(nc.sbuf_tensor(name, list(shape), dtype)), name)


def dram(nc, name, shape, dtype, kind="Internal"):
    return Buf(nc.dram_tensor(name, list(shape), dtype, kind=kind), name)


D = 1024
DIN = 4096
DFF = 4096
DEPTH = 4
LP_FULL = 16384
LS_FULL = 4096
EPS = 1e-6
WIN = 510


def _bc(buf_ap, nparts, mid, inner):
    pstep = buf_ap.ap[0][0]
    return bass.AP(buf_ap.tensor, buf_ap.offset, [[pstep, nparts], [0, mid], [1, inner]])


class K:
    pass


def build(LP, LS, depth, mixers=True):
    nc = bass.Bass("TRN2", target_bir_lowering=False)
    ins = {}
    def inp(name, shape):
        ins[name] = dram(nc, name, shape, F32, "ExternalInput")
        return ins[name]
    seqs = [("p", LP), ("s", LS)]
    xin = {"p": inp("x_prompt", [LP, D]), "s": inp("x_sample", [LS, D])}
    for n, sh in [("norm_mix_pre", [depth, D]), ("norm_mix_post", [depth, D]), ("norm_ffn_pre", [depth, D]),
                  ("norm_ffn_post", [depth, D]), ("w_in", [depth, D, DIN]), ("w_out", [depth, D, D]),
                  ("ffn_w_up", [depth, D, 2 * DFF]), ("ffn_conv_w", [depth, 3, 2 * DFF]),
                  ("ffn_conv_b", [depth, 2 * DFF]), ("ffn_w_down", [depth, DFF, D]),
                  ("hyena_out_norm", [depth, 512]), ("hgrn_out_norm", [depth, 512])]:
        inp(n, sh)
    yout = {"p": dram(nc, "y_prompt", [LP, D], F32, "ExternalOutput"),
            "s": dram(nc, "y_sample", [LS, D], F32, "ExternalOutput")}
    XA = {s: dram(nc, "XA_" + s, [L + 2, D], F32) for s, L in seqs}
    XB = {s: dram(nc, "XB_" + s, [L, D], F32) for s, L in seqs}
    MIX = {s: dram(nc, "MIX_" + s, [D, L], BF16) for s, L in seqs}
    PROJH = {s: dram(nc, "PROJH_" + s, [1536, L + 2], BF16) for s, L in seqs}
    PROJG = {s: dram(nc, "PROJG_" + s, [2560, L], F32) for s, L in seqs}
    WUP = dram(nc, "WUP", [D, 2 * DFF], BF16)
    OFD = {s: dram(nc, "OFD_" + s, [512, L], F32) for s, L in seqs}
    inp("hgrn_lower_bounds", [2, DEPTH, 512])
    C = hyena_inputs(nc, inp, depth, seqs)

    with ExitStack() as es0:
        S = Sched(nc, es0)
        ident = sb(nc, es0, "ident", [128, 128], BF16)
        zrow = sb(nc, es0, "zrow", [128, D], F32)
        with ExitStack() as es:
            idf = sb(nc, es, "idf", [128, 128], F32)
            S.op("pool", lambda e: e.memset(idf[:], 0.0), writes=[idf])
            S.op("pool", lambda e: e.affine_select(out=idf[:], in_=idf[:], pattern=[[-1, 128]], compare_op=ALU.not_equal,
                                                   fill=1.0, base=0, channel_multiplier=1), reads=[idf], writes=[idf])
            S.op("act", lambda e: e.copy(out=ident[:], in_=idf[:]), reads=[idf], writes=[ident])
            S.op("pool", lambda e: e.memset(zrow[:], 0.0), writes=[zrow])
            for s, L in seqs:
                S.dma("sp", XA[s].t.ap()[0:1, :], zrow[0:1, :], reads=[zrow], writes=[XA[s]])
                S.dma("sp", XA[s].t.ap()[L + 1:L + 2, :], zrow[0:1, :], reads=[zrow], writes=[XA[s]])
                zc = sb(nc, es, "zc" + s, [128, 2], BF16)
                S.op("pool", lambda e, zc=zc: e.memset(zc[:], 0.0), writes=[zc])
                for r0 in range(0, 1536, 128):
                    S.dma("sp", PROJH[s].t.ap()[r0:r0 + 128, 0:1], zc[:, 0:1], reads=[zc], writes=[PROJH[s]], allow_slow_non_contiguous=True)
                    S.dma("sp", PROJH[s].t.ap()[r0:r0 + 128, L + 1:L + 2], zc[:, 1:2], reads=[zc], writes=[PROJH[s]], allow_slow_non_contiguous=True)
                if not mixers:
                    zb = sb(nc, es, "zb" + s, [128, 2048], BF16)
                    S.op("pool", lambda e, zb=zb: e.memset(zb[:], 0.0), writes=[zb])
                    for r0 in range(0, D, 128):
                        for c0 in range(0, L, 2048):
                            cwid = min(2048, L - c0)
                            S.dma("sp", MIX[s].t.ap()[r0:r0 + 128, c0:c0 + cwid], zb[:, 0:cwid], reads=[zb], writes=[MIX[s]])
            S.emit()

        def colvec(es, name, src_ap_1d, n):
            t = sb(nc, es, name, [128, n], F32)
            return t, src_ap_1d.rearrange("(k p) -> p k", p=128)

        def norm_window(es, pools, src_rows_ap, nrows, hT, tagbufs):
            nsub = (nrows + 127) // 128
            for j in range(nsub):
                r = min(128, nrows - j * 128)
                xt = pools["xt"].next()
                S.dma("sp", xt[0:r, :], src_rows_ap[j * 128:j * 128 + r, :], reads=tagbufs, writes=[xt])
                junk = pools["junk"].next()
                st = pools["st"].next()
                S.op("act", lambda e, xt=xt, junk=junk, st=st, r=r: e.activation(out=junk[0:r, :], in_=xt[0:r, :], func=AF.Square,
                                                                             accum_out=st[0:r, 0:1]), reads=[xt], writes=[junk, st])
                S.op("act", lambda e, st=st, r=r: e.activation(out=st[0:r, 1:2], in_=st[0:r, 0:1], func=AF.Sqrt,
                                                               scale=1.0 / D, bias=epsc[0:r, :]), reads=[st, epsc], writes=[st])
                S.op("dve", lambda e, st=st, r=r: e.reciprocal(out=st[0:r, 2:3], in_=st[0:r, 1:2]), reads=[st], writes=[st])
                hb = pools["hb"].next()
                S.op("act", lambda e, hb=hb, xt=xt, st=st, r=r: e.activation(out=hb[0:r, :], in_=xt[0:r, :], func=AF.Copy,
                                                                            scale=st[0:r, 2:3]), reads=[xt, st], writes=[hb])
                for kq in range(2):
                    pt = pools["ptr"].next()
                    for kk_ in range(4):
                        k = kq * 4 + kk_
                        S.op("pe", lambda e, pt=pt, hb=hb, k=k, kk_=kk_, r=r: e.transpose(out=pt[:, kk_ * 128:kk_ * 128 + r], in_=hb[0:r, k * 128:(k + 1) * 128],
                                                                                      identity=ident[0:r, 0:r]), reads=[hb, ident], writes=[pt])
                    eng = "dve" if kq == 0 else "act"
                    src = pt[:, :].rearrange("p (a b) -> p a b", a=4)[:, :, 0:r]
                    dst_ = hT[:, kq * 4:(kq + 1) * 4, j * 128:j * 128 + r]
                    if eng == "dve":
                        S.op("dve", lambda e, src=src, dst_=dst_: e.tensor_copy(out=dst_, in_=src), reads=[pt], writes=[hT])
                    else:
                        S.op("act", lambda e, src=src, dst_=dst_: e.copy(out=dst_, in_=src), reads=[pt], writes=[hT])

        epsc = sb(nc, es0, "epsc", [128, 1], F32)
        S.op("pool", lambda e: e.memset(epsc[:], EPS), writes=[epsc])
        onec = sb(nc, es0, "onec", [128, 1], F32)
        S.op("pool", lambda e: e.memset(onec[:], 1.0), writes=[onec])

        for l in range(depth):
            with ExitStack() as es:
                WIN_SB = sb(nc, es, "win_sb", [128, 8, DIN], BF16)
                gcol, gsrc = colvec(es, "gcolA", ins["norm_mix_pre"].t.ap()[l], 8)
                S.dma("sp", gcol[:], gsrc, reads=[ins["norm_mix_pre"]], writes=[gcol], allow_slow_non_contiguous=True)
                stg = Pool(nc, es, "stgA", 2, [128, 2048], F32)
                for k in range(8):
                    for h in range(2):
                        st_ = stg.next()
                        S.dma("sp" if h == 0 else "pool", st_[:], ins["w_in"].t.ap()[l, k * 128:(k + 1) * 128, h * 2048:(h + 1) * 2048],
                              reads=[ins["w_in"]], writes=[st_])
                        S.op("act" if h == 0 else "dve",
                             (lambda e, st_=st_, k=k, h=h: e.activation(out=WIN_SB[:, k, h * 2048:(h + 1) * 2048], in_=st_[:], func=AF.Copy,
                                                                      scale=gcol[:, k:k + 1])) if h == 0 else
                             (lambda e, st_=st_, k=k, h=h: e.tensor_scalar(out=WIN_SB[:, k, h * 2048:(h + 1) * 2048], in0=st_[:],
                                                                         scalar1=gcol[:, k:k + 1], scalar2=None, op0=ALU.mult)),
                             reads=[st_, gcol], writes=[WIN_SB])
                pools = {"xt": Pool(nc, es, "xtA", 2, [128, D], F32), "junk": Pool(nc, es, "junkA", 1, [128, D], F32),
                         "st": Pool(nc, es, "stA", 4, [128, 4], F32), "hb": Pool(nc, es, "hbA", 2, [128, D], BF16),
                         "ptr": Pool(nc, es, "ptrA", 2, [128, 512], BF16, space="psum")}
                hTp = Pool(nc, es, "hTA", 2, [128, 8, 512], BF16)
                psA = Pool(nc, es, "psA", 4, [128, 512], F32, space="psum")
                oh = Pool(nc, es, "ohA", 3, [128, 512], BF16)
                og = Pool(nc, es, "ogA", 3, [128, 512], F32)
                for s, L in seqs:
                    src = xin[s] if l == 0 else XB[s]
                    for t0 in range(0, L, 512):
                        n = min(512, L - t0)
                        hT = hTp.next()
                        norm_window(es, pools, src.t.ap()[t0:t0 + n, :], n, hT, [src])
                        if not mixers:
                            continue
                        for m in range(32):
                            ps = psA.next()
                            for k in range(8):
                                S.op("pe", lambda e, ps=ps, k=k, m=m, hT=hT, n=n: e.matmul(out=ps[:, 0:n], lhsT=WIN_SB[:, k, m * 128:(m + 1) * 128],
                                                                                          rhs=hT[:, k, 0:n], start=(k == 0), stop=(k == 7)),
                                     reads=[WIN_SB, hT], writes=[ps])
                            if m < 12:
                                o = oh.next()
                                S.op("act" if m % 2 else "dve", (lambda e, o=o, ps=ps, n=n: e.copy(out=o[:, 0:n], in_=ps[:, 0:n])) if m % 2 else
                                     (lambda e, o=o, ps=ps, n=n: e.tensor_copy(out=o[:, 0:n], in_=ps[:, 0:n])), reads=[ps], writes=[o])
                                S.dma("pool", PROJH[s].t.ap()[m * 128:(m + 1) * 128, 1 + t0:1 + t0 + n], o[:, 0:n], reads=[o], writes=[PROJH[s]])
                            else:
                                o = og.next()
                                S.op("act" if m % 2 else "dve", (lambda e, o=o, ps=ps, n=n: e.copy(out=o[:, 0:n], in_=ps[:, 0:n])) if m % 2 else
                                     (lambda e, o=o, ps=ps, n=n: e.tensor_copy(out=o[:, 0:n], in_=ps[:, 0:n])), reads=[ps], writes=[o])
                                S.dma("pool", PROJG[s].t.ap()[(m - 12) * 128:(m - 11) * 128, t0:t0 + n], o[:, 0:n], reads=[o], writes=[PROJG[s]])
                S.emit()

            if mixers:
                hgrn_phase(nc, S, l, depth, seqs, ins, PROJG, MIX, OFD, ident, epsc)
                hyena_phase(nc, S, l, depth, seqs, ins, PROJH, MIX, ident, epsc, C)

            with ExitStack() as es:
                WO = sb(nc, es, "wo_sb", [128, 8, D], BF16)
                WDN = sb(nc, es, "wdn_sb", [128, 32, D], BF16)
                gco = sb(nc, es, "gco", [128, 8], F32)
                S.dma("sp", gco[:, 0:4], ins["hyena_out_norm"].t.ap()[l].rearrange("(k p) -> p k", p=128), reads=[ins["hyena_out_norm"]],
                      writes=[gco], allow_slow_non_contiguous=True)
                S.dma("sp", gco[:, 4:8], ins["hgrn_out_norm"].t.ap()[l].rearrange("(k p) -> p k", p=128), reads=[ins["hgrn_out_norm"]],
                      writes=[gco], allow_slow_non_contiguous=True)
                gcf, gsrc = colvec(es, "gcf", ins["norm_ffn_pre"].t.ap()[l], 8)
                S.dma("sp", gcf[:], gsrc, reads=[ins["norm_ffn_pre"]], writes=[gcf], allow_slow_non_contiguous=True)
                cw = sb(nc, es, "cw", [128, 4, 64], F32)
                for tp in range(3):
                    S.dma("sp", cw[:, tp, :], ins["ffn_conv_w"].t.ap()[l, tp].rearrange("(k p) -> p k", p=128), reads=[ins["ffn_conv_w"]],
                          writes=[cw], allow_slow_non_contiguous=True)
                S.dma("sp", cw[:, 3, :], ins["ffn_conv_b"].t.ap()[l].rearrange("(k p) -> p k", p=128), reads=[ins["ffn_conv_b"]],
                      writes=[cw], allow_slow_non_contiguous=True)
                GPM = sb(nc, es, "gpm", [128, D], F32)
                GPF = sb(nc, es, "gpf", [128, D], F32)
                S.dma("sp", GPM[:], bass.AP(ins["norm_mix_post"].t, l * D, [[0, 128], [1, D]]), reads=[ins["norm_mix_post"]], writes=[GPM])
                S.dma("sp", GPF[:], bass.AP(ins["norm_ffn_post"].t, l * D, [[0, 128], [1, D]]), reads=[ins["norm_ffn_post"]], writes=[GPF])
                pools = {"xt": Pool(nc, es, "xtC", 2, [128, D], F32), "junk": Pool(nc, es, "junkC", 1, [128, D], F32),
                         "st": Pool(nc, es, "stC", 4, [128, 4], F32), "hb": Pool(nc, es, "hbC", 2, [128, D], BF16),
                         "ptr": Pool(nc, es, "ptrC", 2, [128, 512], BF16, space="psum")}
                psO = Pool(nc, es, "psO", 1, [128, D], F32, space="psum")
                tmpf = Pool(nc, es, "tmpf", 2, [128, D], F32)
                esC0 = ExitStack()
                stg = Pool(nc, esC0, "stgC", 2, [128, 2048], F32)
                for k in range(8):
                    st_ = stg.next()
                    S.dma("sp", st_[:, 0:D], ins["w_out"].t.ap()[l, k * 128:(k + 1) * 128, :], reads=[ins["w_out"]], writes=[st_])
                    S.op("act", lambda e, st_=st_, k=k: e.activation(out=WO[:, k, :], in_=st_[:, 0:D], func=AF.Copy, scale=gco[:, k:k + 1]),
                         reads=[st_, gco], writes=[WO])
                for k in range(0, 32, 2):
                    st_ = stg.next()
                    S.dma("sp", st_[:].rearrange("p (a b) -> p a b", a=2),
                          ins["ffn_w_down"].t.ap()[l, k * 128:(k + 2) * 128, :].rearrange("(a p) n -> p a n", p=128),
                          reads=[ins["ffn_w_down"]], writes=[st_])
                    S.op("dve", lambda e, st_=st_, k=k: e.tensor_copy(out=WDN[:, k:k + 2, :].rearrange("p a b -> p (a b)"), in_=st_[:]),
                         reads=[st_], writes=[WDN])
                wst = Pool(nc, esC0, "wstC", 2, [128, 2048], BF16)
                for k in range(8):
                    for h in range(4):
                        st_ = stg.next()
                        S.dma("sp", st_[:], ins["ffn_w_up"].t.ap()[l, k * 128:(k + 1) * 128, h * 2048:(h + 1) * 2048], reads=[ins["ffn_w_up"]], writes=[st_])
                        wb = wst.next()
                        S.op("act" if h % 2 else "dve",
                             (lambda e, st_=st_, wb=wb, k=k: e.activation(out=wb[:], in_=st_[:], func=AF.Copy, scale=gcf[:, k:k + 1])) if h % 2 else
                             (lambda e, st_=st_, wb=wb, k=k: e.tensor_scalar(out=wb[:], in0=st_[:], scalar1=gcf[:, k:k + 1], scalar2=None, op0=ALU.mult)),
                             reads=[st_, gcf], writes=[wb])
                        S.dma("pool", WUP.t.ap()[k * 128:(k + 1) * 128, h * 2048:(h + 1) * 2048], wb[:], reads=[wb], writes=[WUP])
                S.emit()
                esC0.close()
                esC1 = ExitStack()
                mixT = Pool(nc, esC1, "mixT", 2, [128, 8, 512], BF16)

                for s, L in seqs:
                    src = xin[s] if l == 0 else XB[s]
                    for t0 in range(0, L, 512):
                        n = min(512, L - t0)
                        mt = mixT.next()
                        S.dma("sp", mt[:, :, 0:n], MIX[s].t.ap()[:, t0:t0 + n].rearrange("(k p) n -> p k n", p=128), reads=[MIX[s]], writes=[mt])
                        for j in range((n + 127) // 128):
                            r = min(128, n - j * 128)
                            ps = psO.next()
                            for nb in range(2):
                                for k in range(8):
                                    S.op("pe", lambda e, ps=ps, mt=mt, k=k, nb=nb, j=j, r=r: e.matmul(
                                        out=ps[0:r, nb * 512:(nb + 1) * 512], lhsT=mt[:, k, j * 128:j * 128 + r], rhs=WO[:, k, nb * 512:(nb + 1) * 512],
                                        start=(k == 0), stop=(k == 7)), reads=[mt, WO], writes=[ps])
                            residual_update(S, pools, tmpf, ps, r, GPM, src.t.ap()[t0 + j * 128:t0 + j * 128 + r, :], src,
                                            XA[s].t.ap()[1 + t0 + j * 128:1 + t0 + j * 128 + r, :], XA[s], epsc)
                S.emit()
                esC1.close()
                hTp = Pool(nc, es, "hTC", 2, [128, 8, 512], BF16)
                G = sb(nc, es, "Gffn", [128, 32, 512], BF16)
                wup = Pool(nc, es, "wupC", 3, [128, 8, 256], BF16)
                psU = Pool(nc, es, "psU", 4, [128, 512], F32, space="psum")
                cva = Pool(nc, es, "cva2", 4, [128, 512], F32)
                cvb = Pool(nc, es, "cvb2", 4, [128, 512], F32)
                gt = Pool(nc, es, "gt2", 3, [128, 512], F32)
                wins = []
                for s, L in seqs:
                    dst = yout[s] if l == depth - 1 else XB[s]
                    for p0 in range(0, L, WIN):
                        wins.append((s, L, dst, p0, min(WIN, L - p0)))
                hts = {}

                def prep(i):
                    s, L, dst, p0, n = wins[i]
                    hT = hTp.next()
                    norm_window(es, pools, XA[s].t.ap()[p0:p0 + n + 2, :], n + 2, hT, [XA[s]])
                    hts[i] = hT

                prep(0)
                for wi, (s, L, dst, p0, n) in enumerate(wins):
                    nw = n + 2
                    hT = hts.pop(wi)
                    st8 = {}

                    def s0(j, hT=hT, nw=nw, st8=st8):
                        wt = wup.next()
                        S.dma("sp", wt[:, :, 0:128], WUP.t.ap()[:, j * 128:(j + 1) * 128].rearrange("(k p) n -> p k n", p=128), reads=[WUP], writes=[wt])
                        S.dma("sp", wt[:, :, 128:256], WUP.t.ap()[:, DFF + j * 128:DFF + (j + 1) * 128].rearrange("(k p) n -> p k n", p=128),
                              reads=[WUP], writes=[wt])
                        pa = psU.next()
                        pb = psU.next()
                        for half, ps in ((0, pa), (1, pb)):
                            for k in range(8):
                                S.op("pe", lambda e, ps=ps, wt=wt, k=k, half=half: e.matmul(
                                    out=ps[:, 0:nw], lhsT=wt[:, k, half * 128:(half + 1) * 128], rhs=hT[:, k, 0:nw], start=(k == 0), stop=(k == 7)),
                                    reads=[wt, hT], writes=[ps])
                        st8[j] = dict(pa=pa, pb=pb)

                    def s1(j, n=n, st8=st8):
                        d_ = st8[j]
                        d_["ca"], d_["cb"] = cva.next(), cvb.next()
                        trip = ((d_["pa"], d_["ca"], j), (d_["pb"], d_["cb"], 32 + j))
                        for ps, c_, ch in trip:
                            S.op("act", lambda e, ps=ps, c_=c_, ch=ch: e.activation(out=c_[:, 0:n], in_=ps[:, 0:n], func=AF.Identity,
                                                                                 scale=cw[:, 0, ch:ch + 1], bias=cw[:, 3, ch:ch + 1]), reads=[ps, cw], writes=[c_])
                        for tp in (1, 2):
                            for ps, c_, ch in trip:
                                S.op("dve", lambda e, ps=ps, c_=c_, ch=ch, tp=tp: e.scalar_tensor_tensor(out=c_[:, 0:n], in0=ps[:, tp:n + tp], scalar=cw[:, tp, ch:ch + 1],
                                                                                                      in1=c_[:, 0:n], op0=ALU.mult, op1=ALU.add), reads=[ps, cw, c_], writes=[c_])

                    def s2(j, n=n, st8=st8):
                        d_ = st8[j]
                        g1 = d_["g1"] = gt.next()
                        ca = d_["ca"]
                        S.op("act", lambda e: e.activation(out=g1[:, 0:n], in_=ca[:, 0:n], func=AF.Square), reads=[ca], writes=[g1])
                        S.op("dve", lambda e: e.tensor_scalar(out=g1[:, 0:n], in0=g1[:, 0:n], scalar1=0.044715, scalar2=1.0, op0=ALU.mult, op1=ALU.add),
                             reads=[g1], writes=[g1])
                        S.op("dve", lambda e: e.tensor_tensor(out=g1[:, 0:n], in0=g1[:, 0:n], in1=ca[:, 0:n], op=ALU.mult), reads=[g1, ca], writes=[g1])

                    def s3(j, n=n, st8=st8):
                        d_ = st8.pop(j)
                        g1, ca, cb = d_["g1"], d_["ca"], d_["cb"]
                        S.op("act", lambda e: e.activation(out=g1[:, 0:n], in_=g1[:, 0:n], func=AF.Sigmoid, scale=1.5957691216057308), reads=[g1], writes=[g1])
                        S.op("pool", lambda e: e.tensor_tensor(out=cb[:, 0:n], in0=ca[:, 0:n], in1=cb[:, 0:n], op=ALU.mult), reads=[ca, cb], writes=[cb])
                        S.op("pool", lambda e: e.tensor_tensor(out=G[:, j, 0:n], in0=g1[:, 0:n], in1=cb[:, 0:n], op=ALU.mult), reads=[g1, cb], writes=[G])

                    stgs = (s0, s1, s2, s3)
                    for t_ in range(32 + 3):
                        for k_ in (3, 2, 1, 0):
                            if 0 <= t_ - k_ < 32:
                                stgs[k_](t_ - k_)
                    if wi + 1 < len(wins):
                        prep(wi + 1)
                    for jj in range((n + 127) // 128):
                        r = min(128, n - jj * 128)
                        ps = psO.next()
                        for nb in range(2):
                            for k in range(32):
                                S.op("pe", lambda e, ps=ps, k=k, nb=nb, jj=jj, r=r: e.matmul(
                                    out=ps[0:r, nb * 512:(nb + 1) * 512], lhsT=G[:, k, jj * 128:jj * 128 + r], rhs=WDN[:, k, nb * 512:(nb + 1) * 512],
                                    start=(k == 0), stop=(k == 31)), reads=[G, WDN], writes=[ps])
                        tok = p0 + jj * 128
                        residual_update(S, pools, tmpf, ps, r, GPF, XA[s].t.ap()[1 + tok:1 + tok + r, :], XA[s],
                                        dst.t.ap()[tok:tok + r, :], dst, epsc)
                S.emit()
    global LAST_NINST
    LAST_NINST = S.ninst
    return nc, list(ins.keys())


def make_in_map(inputs, names, r, LP, LS, depth):
    consts = {"p": hy_consts(LP), "s": hy_consts(LS)}
    m = {}
    for n in names:
        if n.startswith("hc_"):
            _, s, k = n.split("_", 2)
            a = consts[s][k]
        else:
            a = np.asarray(inputs[n], dtype=np.float32)
            if n == "x_prompt":
                a = a[0, :LP]
            elif n == "x_sample":
                a = a[r, :LS]
            elif n == "hgrn_lower_bounds":
                a = a
            else:
                a = a[:depth]
        m[n] = np.ascontiguousarray(a, dtype=np.float32)
    return m


def residual_update(S, pools, tmpf, ps, r, GP, xsrc_ap, xsrc_buf, xdst_ap, xdst_buf, epsc):
    junk = pools["junk"].next()
    st = pools["st"].next()
    S.op("act", lambda e: e.activation(out=junk[0:r, :], in_=ps[0:r, :], func=AF.Square, accum_out=st[0:r, 0:1]), reads=[ps], writes=[junk, st])
    S.op("act", lambda e: e.activation(out=st[0:r, 1:2], in_=st[0:r, 0:1], func=AF.Sqrt, scale=1.0 / D, bias=epsc[0:r, :]), reads=[st, epsc], writes=[st])
    S.op("dve", lambda e: e.reciprocal(out=st[0:r, 2:3], in_=st[0:r, 1:2]), reads=[st], writes=[st])
    t = tmpf.next()
    xt = pools["xt"].next()
    S.dma("sp", xt[0:r, :], xsrc_ap, reads=[xsrc_buf], writes=[xt])
    S.op("dve", lambda e: e.scalar_tensor_tensor(out=t[0:r, :], in0=ps[0:r, :], scalar=st[0:r, 2:3], in1=GP[0:r, :], op0=ALU.mult, op1=ALU.mult),
         reads=[ps, st, GP], writes=[t])
    S.op("pool", lambda e: e.tensor_tensor(out=t[0:r, :], in0=t[0:r, :], in1=xt[0:r, :], op=ALU.add), reads=[t, xt], writes=[t])
    S.dma("pool", xdst_ap, t[0:r, :], reads=[t], writes=[xdst_buf])


def hgrn_phase(nc, S, l, depth, seqs, ins, PROJG, MIX, OFD, ident, epsc):
    with ExitStack() as es:
        RM = sb(nc, es, "hg_rm", [128, 512], F32)
        S.op("pool", lambda e: e.memset(RM[:], 1.0), writes=[RM])
        S.op("pool", lambda e: e.memset(RM[:, 0:512:64], 0.0), reads=[RM], writes=[RM])
        MASKf = sb(nc, es, "hg_maskf", [128, 128], F32)
        S.op("pool", lambda e: e.memset(MASKf[:], 1.0), writes=[MASKf])
        S.op("pool", lambda e: e.affine_select(out=MASKf[:], in_=MASKf[:], pattern=[[1, 128]], compare_op=ALU.is_ge, fill=0.0, base=0,
                                               channel_multiplier=-1), reads=[MASKf], writes=[MASKf])
        S.op("pool", lambda e: e.memset(MASKf[0:64, 64:128], 0.0), reads=[MASKf], writes=[MASKf])
        ONES = sb(nc, es, "hg_ones", [128, 128], BF16)
        S.op("pool", lambda e: e.memset(ONES[:], 1.0), writes=[ONES])
        tb = sb(nc, es, "hg_tb", [128, 2, 4, DEPTH], F32)
        for d_ in range(2):
            for h in range(4):
                S.dma("sp", tb[:, d_, h, :], ins["hgrn_lower_bounds"].t.ap()[d_, :, h * 128:(h + 1) * 128].rearrange("l p -> p l"),
                      reads=[ins["hgrn_lower_bounds"]], writes=[tb], allow_slow_non_contiguous=True)
        te = sb(nc, es, "hg_te", [128, 8, DEPTH], F32)
        S.op("act", lambda e: e.activation(out=te[:], in_=tb[:].rearrange("p a b c -> p (a b) c"), func=AF.Exp), reads=[tb], writes=[te])
        tsum = sb(nc, es, "hg_ts", [128, 8, 3], F32)
        S.op("dve", lambda e: e.tensor_reduce(out=tsum[:, :, 0], in_=te[:], axis=AX.X, op=ALU.add), reads=[te], writes=[tsum])
        if l >= 1:
            S.op("dve", lambda e: e.tensor_reduce(out=tsum[:, :, 1], in_=te[:, :, 1:l + 1], axis=AX.X, op=ALU.add), reads=[te], writes=[tsum])
        else:
            S.op("dve", lambda e: e.memset(tsum[:, :, 1], 0.0), reads=[tsum], writes=[tsum])
        S.op("dve", lambda e: e.reciprocal(out=tsum[:, :, 2], in_=tsum[:, :, 0]), reads=[tsum], writes=[tsum])
        LB = sb(nc, es, "hg_lb", [128, 8, 2], F32)
        S.op("dve", lambda e: e.tensor_tensor(out=LB[:, :, 0], in0=tsum[:, :, 1], in1=tsum[:, :, 2], op=ALU.mult), reads=[tsum], writes=[LB])
        S.op("dve", lambda e: e.tensor_scalar(out=LB[:, :, 1], in0=LB[:, :, 0], scalar1=-1.0, scalar2=1.0, op0=ALU.mult, op1=ALU.add),
             reads=[LB], writes=[LB])

        NH = 4
        ld = {n: Pool(nc, es, "hg_ld" + n, NH, [128, 512], F32) for n in ("f", "q", "i")}
        ldg = Pool(nc, es, "hg_ldg", NH, [128, 512], F32)
        ldo = Pool(nc, es, "hg_ldo", NH, [128, 512], F32)
        w32 = {n: Pool(nc, es, "hg_" + n, NH, [128, 512], F32) for n in ("u", "gl", "kk", "E1", "E2", "E3", "X1", "X2", "X2n", "X3")}
        wbf = {n: Pool(nc, es, "hg_" + n, NH, [128, 512], BF16) for n in ("qh", "qt", "kt", "kh", "vb")}
        OFp = Pool(nc, es, "hg_of", NH, [128, 512], F32)
        SQp = Pool(nc, es, "hg_sq", 2, [128, 512], BF16)
        OBp = Pool(nc, es, "hg_ob", 2, [128, 512], BF16)
        khTp = Pool(nc, es, "hg_khT", NH + 1, [128, 128], BF16)
        vTp = Pool(nc, es, "hg_vT", NH + 1, [128, 128], BF16)
        scmp = Pool(nc, es, "hg_scm", NH + 1, [128, 128], BF16)
        Sbfp = Pool(nc, es, "hg_sbf", 3 * NH, [128, 128], BF16)
        S32s = [sb(nc, es, f"hg_s32_{h}", [128, 128], F32) for h in range(NH)]
        ptr = Pool(nc, es, "hg_ptr", 2, [128, 128], BF16, space="psum")
        psc = Pool(nc, es, "hg_psc", 1, [128, 128], F32, space="psum")
        pkv = Pool(nc, es, "hg_pkv", 2, [128, 128], F32, space="psum")
        po = Pool(nc, es, "hg_po", 2, [128, 128], F32, space="psum")
        pss = Pool(nc, es, "hg_pss", 1, [128, 512], F32, space="psum")

        def V(ap, rev):
            return ap[:, ::-1] if rev else ap

        def bc(t, off):
            a = t[:]
            return bass.AP(a.tensor, a.offset + off, [[a.ap[0][0], 128], [64, 8], [0, 64]])

        def v3(t):
            return t[:].rearrange("p (c j) -> p c j", j=64)

        for s, L in seqs:
            nst = L // 512
            for dr in range(2):
                rev = dr == 1
                for h in range(NH):
                    S.op("pool", lambda e, h=h: e.memset(S32s[h][:], 0.0), reads=[S32s[h]], writes=[S32s[h]])
                order = range(nst - 1, -1, -1) if rev else range(nst)
                for st in order:
                    t0 = st * 512
                    hs = []
                    for h in range(NH):
                        fT, qT, iT = ld["f"].next(), ld["q"].next(), ld["i"].next()
                        S.dma("sp", fT[:], PROJG[s].t.ap()[512 + dr * 512 + h * 128:512 + dr * 512 + (h + 1) * 128, t0:t0 + 512], reads=[PROJG[s]], writes=[fT])
                        S.dma("sp", qT[:], PROJG[s].t.ap()[h * 128:(h + 1) * 128, t0:t0 + 512], reads=[PROJG[s]], writes=[qT])
                        S.dma("sp", iT[:], PROJG[s].t.ap()[1536 + h * 128:1536 + (h + 1) * 128, t0:t0 + 512], reads=[PROJG[s]], writes=[iT])
                        H = dict(rev=rev, li=dr * 4 + h, fT=fT, qT=qT, iT=iT, S32=S32s[h], OF=OFp.next(), OFL=None, gT=None)
                        for n_ in ("u", "gl", "kk", "E1", "E2", "E3", "X1", "X2", "X2n", "X3"):
                            H[n_] = w32[n_].next()
                        for n_ in ("qh", "qt", "kt", "kh", "vb"):
                            H[n_] = wbf[n_].next()
                        H["qs"] = H["u"]
                        if rev:
                            H["OFL"] = ldo.next()
                            H["gT"] = ldg.next()
                            S.dma("sp", H["OFL"][:], OFD[s].t.ap()[h * 128:(h + 1) * 128, t0:t0 + 512], reads=[OFD[s]], writes=[H["OFL"]])
                            S.dma("sp", H["gT"][:], PROJG[s].t.ap()[2048 + h * 128:2048 + (h + 1) * 128, t0:t0 + 512], reads=[PROJG[s]], writes=[H["gT"]])
                        hs.append(H)
                    stages = [
                        lambda H: S.op("act", lambda e: e.activation(out=H["u"][:], in_=H["fT"][:], func=AF.Sigmoid), reads=[H["fT"]], writes=[H["u"]]),
                        lambda H: S.op("act", lambda e: e.activation(out=H["u"][:], in_=H["u"][:], func=AF.Identity, scale=LB[:, H["li"], 1:2],
                                                                     bias=LB[:, H["li"], 0:1]), reads=[H["u"], LB], writes=[H["u"]]),
                        lambda H: (S.op("act", lambda e: e.activation(out=H["gl"][:], in_=H["u"][:], func=AF.Ln), reads=[H["u"]], writes=[H["gl"]]),
                                   S.op("pool", lambda e: e.tensor_scalar(out=H["kk"][:], in0=H["u"][:], scalar1=-1.0, scalar2=1.0, op0=ALU.mult, op1=ALU.add),
                                        reads=[H["u"]], writes=[H["kk"]]),
                                   S.op("pool", lambda e: e.tensor_copy(out=H["vb"][:], in_=V(H["iT"][:], H["rev"])), reads=[H["iT"]], writes=[H["vb"]])),
                        lambda H: (S.op("dve", lambda e: e.tensor_tensor_scan(out=H["E1"][:], data0=RM[:], data1=V(H["gl"][:], H["rev"]), initial=0.0,
                                                                             op0=ALU.mult, op1=ALU.add), reads=[H["gl"], RM], writes=[H["E1"]]),
                                   S.op("act", lambda e: e.activation(out=H["qs"][:], in_=V(H["qT"][:], H["rev"]), func=AF.Silu), reads=[H["qT"], H["kk"], H["gl"]],
                                        writes=[H["qs"]])),
                        lambda H: (S.op("dve", lambda e: e.tensor_tensor(out=v3(H["E2"]), in0=v3(H["E1"]), in1=bc(H["E1"], 31), op=ALU.subtract),
                                        reads=[H["E1"]], writes=[H["E2"]]),
                                   S.op("pool", lambda e: e.tensor_tensor(out=v3(H["E3"]), in0=bc(H["E1"], 63), in1=v3(H["E1"]), op=ALU.subtract),
                                        reads=[H["E1"]], writes=[H["E3"]]),
                                   S.op("act", lambda e: e.activation(out=H["X1"][:], in_=H["E1"][:], func=AF.Exp), reads=[H["E1"]], writes=[H["X1"]])),
                        lambda H: (S.op("act", lambda e: e.activation(out=H["X2"][:], in_=H["E2"][:], func=AF.Exp), reads=[H["E2"]], writes=[H["X2"]]),
                                   S.op("act", lambda e: e.activation(out=H["X2n"][:], in_=H["E2"][:], func=AF.Exp, scale=-1.0), reads=[H["E2"]], writes=[H["X2n"]]),
                                   S.op("act", lambda e: e.activation(out=H["X3"][:], in_=H["E3"][:], func=AF.Exp), reads=[H["E3"]], writes=[H["X3"]]),
                                   S.op("dve", lambda e: e.tensor_tensor(out=H["qh"][:], in0=H["qs"][:], in1=H["X1"][:], op=ALU.mult), reads=[H["qs"], H["X1"]],
                                        writes=[H["qh"]])),
                        lambda H: (S.op("pool", lambda e: e.tensor_tensor(out=H["qt"][:], in0=H["qs"][:], in1=H["X2"][:], op=ALU.mult), reads=[H["qs"], H["X2"]],
                                        writes=[H["qt"]]),
                                   S.op("dve", lambda e: e.tensor_tensor(out=H["kt"][:], in0=V(H["kk"][:], H["rev"]), in1=H["X2n"][:], op=ALU.mult),
                                        reads=[H["kk"], H["X2n"]], writes=[H["kt"]]),
                                   S.op("pool", lambda e: e.tensor_tensor(out=H["kh"][:], in0=V(H["kk"][:], H["rev"]), in1=H["X3"][:], op=ALU.mult),
                                        reads=[H["kk"], H["X3"]], writes=[H["kh"]])),
                    ]
                    for stg_ in stages:
                        for H in hs:
                            stg_(H)
                    for j in range(4):
                        cs = slice(j * 128, (j + 1) * 128)
                        for H in hs:
                            p1 = ptr.next()
                            S.op("pe", lambda e, p1=p1, kh=H["kh"], cs=cs: e.transpose(out=p1[:], in_=kh[:, cs], identity=ident[:]), reads=[H["kh"], ident], writes=[p1])
                            khT = khTp.next()
                            S.op("dve", lambda e, p1=p1, khT=khT: e.tensor_copy(out=khT[:], in_=p1[:]), reads=[p1], writes=[khT])
                            p2 = ptr.next()
                            S.op("pe", lambda e, p2=p2, vb=H["vb"], cs=cs: e.transpose(out=p2[:], in_=vb[:, cs], identity=ident[:]), reads=[H["vb"], ident], writes=[p2])
                            vT = vTp.next()
                            S.op("act", lambda e, p2=p2, vT=vT: e.copy(out=vT[:], in_=p2[:]), reads=[p2], writes=[vT])
                            sc = psc.next()
                            S.op("pe", lambda e, sc=sc, kt=H["kt"], qt=H["qt"], cs=cs: e.matmul(out=sc[:], lhsT=kt[:, cs], rhs=qt[:, cs], start=True, stop=True),
                                 reads=[H["kt"], H["qt"]], writes=[sc])
                            scm = scmp.next()
                            S.op("dve", lambda e, sc=sc, scm=scm: e.tensor_tensor(out=scm[:], in0=sc[:], in1=MASKf[:], op=ALU.mult), reads=[sc, MASKf], writes=[scm])
                            H["khT"], H["vT"], H["scm"] = khT, vT, scm
                        for c in range(2):
                            for H in hs:
                                Sbf = Sbfp.next()
                                H["Sbf%d" % c] = Sbf
                                S32 = H["S32"]
                                S.op("act", lambda e, Sbf=Sbf, S32=S32: e.copy(out=Sbf[:], in_=S32[:]), reads=[S32], writes=[Sbf])
                                ccol = j * 128 + c * 64
                                kv = pkv.next()
                                S.op("pe", lambda e, kv=kv, khT=H["khT"], vT=H["vT"], c=c: e.matmul(out=kv[:], lhsT=khT[c * 64:(c + 1) * 64, :],
                                                                                                   rhs=vT[c * 64:(c + 1) * 64, :], start=True, stop=True),
                                     reads=[H["khT"], H["vT"]], writes=[kv])
                                xb_col = ccol + 63
                                S.op("dve", lambda e, kv=kv, X1=H["X1"], xb_col=xb_col, S32=S32: e.scalar_tensor_tensor(
                                    out=S32[:], in0=S32[:], scalar=X1[:, xb_col:xb_col + 1], in1=kv[:], op0=ALU.mult, op1=ALU.add),
                                    reads=[S32, H["X1"], kv], writes=[S32])
                        for H in hs:
                            o_ps = po.next()
                            S.op("pe", lambda e, o_ps=o_ps, vT=H["vT"], scm=H["scm"]: e.matmul(out=o_ps[:], lhsT=vT[:], rhs=scm[:], start=True, stop=False),
                                 reads=[H["vT"], H["scm"]], writes=[o_ps])
                            for c in range(2):
                                ccol = j * 128 + c * 64
                                Sbf = H["Sbf%d" % c]
                                S.op("pe", lambda e, o_ps=o_ps, Sbf=Sbf, qh=H["qh"], c=c, ccol=ccol: e.matmul(
                                    out=o_ps[:, c * 64:(c + 1) * 64], lhsT=Sbf[:], rhs=qh[:, ccol:ccol + 64], start=False, stop=(c == 1)),
                                    reads=[Sbf, H["qh"]], writes=[o_ps])
                            if not rev:
                                S.op("act", lambda e, OF=H["OF"], o_ps=o_ps, cs=cs: e.copy(out=OF[:, cs], in_=o_ps[:]), reads=[o_ps], writes=[H["OF"]])
                            else:
                                S.op("dve", lambda e, OF=H["OF"], o_ps=o_ps, cs=cs, OFL=H["OFL"]: e.tensor_tensor(
                                    out=OF[:, ::-1][:, cs], in0=o_ps[:], in1=OFL[:, ::-1][:, cs], op=ALU.add), reads=[o_ps, H["OFL"]], writes=[H["OF"]])
                    for h, H in enumerate(hs):
                        OF = H["OF"]
                        if not rev:
                            S.dma("sp", OFD[s].t.ap()[h * 128:(h + 1) * 128, t0:t0 + 512], OF[:], reads=[OF], writes=[OFD[s]])
                        else:
                            SQ, OB, gT = SQp.next(), OBp.next(), H["gT"]
                            RS = H["OFL"]
                            S.op("act", lambda e, SQ=SQ, OF=OF: e.activation(out=SQ[:], in_=OF[:], func=AF.Square), reads=[OF], writes=[SQ])
                            ss = pss.next()
                            S.op("pe", lambda e, ss=ss, SQ=SQ: e.matmul(out=ss[:], lhsT=ONES[:], rhs=SQ[:], start=True, stop=True), reads=[ONES, SQ], writes=[ss])
                            S.op("act", lambda e, ss=ss, RS=RS: e.activation(out=RS[:], in_=ss[:], func=AF.Sqrt, scale=1.0 / 128, bias=epsc[:]),
                                 reads=[ss, epsc], writes=[RS])
                            S.op("dve", lambda e, RS=RS: e.reciprocal(out=RS[:], in_=RS[:]), reads=[RS], writes=[RS])
                            S.op("act", lambda e, gT=gT: e.activation(out=gT[:], in_=gT[:], func=AF.Silu), reads=[gT], writes=[gT])
                            S.op("dve", lambda e, RS=RS, OF=OF: e.tensor_tensor(out=RS[:], in0=RS[:], in1=OF[:], op=ALU.mult), reads=[RS, OF], writes=[RS])
                            S.op("pool", lambda e, RS=RS, gT=gT, OB=OB: e.tensor_tensor(out=OB[:], in0=RS[:], in1=gT[:], op=ALU.mult), reads=[RS, gT], writes=[OB])
                            S.dma("sp", MIX[s].t.ap()[512 + h * 128:512 + (h + 1) * 128, t0:t0 + 512], OB[:], reads=[OB], writes=[MIX[s]])
        S.emit()


LAST_NINST = 0
MAGIC = 12582912.0
TWO_PI = 6.283185


def hy_consts(L):
    B = L // 128
    n1 = np.arange(128)[:, None]
    k1 = np.arange(128)[None, :]
    th = 2 * np.pi * n1 * (k1 + 0.5) / 256
    C = {}
    C["F1CAT"] = np.concatenate([np.cos(th), -np.sin(th)], axis=1)
    n2 = np.arange(B)[:, None]
    ph = np.pi * n2 * (k1 + 0.5) / L
    C["TWR"] = np.cos(ph)
    C["TWI"] = -np.sin(ph)
    k2 = np.arange(B)[None, :]
    a2 = 2 * np.pi * n2 * k2 / B
    C["F2RE"] = np.cos(a2)
    C["F2IM"] = -np.sin(a2)
    C["F2IMN"] = np.sin(a2)
    fr, fi = np.cos(a2), np.sin(a2)
    C["F2I_A"] = np.concatenate([fr, fi], axis=1)
    C["F2I_B"] = np.concatenate([-fi, fr], axis=1)
    phi = np.pi * (np.arange(128)[:, None] + 0.5) * np.arange(B)[None, :] / L
    C["ITWR"] = np.cos(phi)
    C["ITWI"] = np.sin(phi)
    thi = 2 * np.pi * (np.arange(128)[:, None] + 0.5) * np.arange(128)[None, :] / 256
    C["F1I_RE"] = np.cos(thi) / L
    C["F1I_IM"] = -np.sin(thi) / L
    t = np.linspace(0.0, 1.0, L, dtype=np.float32)[:, None]
    ang = (np.float32(2.0 * math.pi / L) * np.arange(L, dtype=np.float32))[:, None]
    bands = np.linspace(1e-4, 15, 16, dtype=np.float32)[None, :]
    feats = np.concatenate([t, np.cos(bands * ang), -np.sin(bands * ang)], axis=-1)
    C["FEAT"] = feats.T
    deltas = np.abs(np.linspace(math.log(1e-2) / 1.5, math.log(1e-2) / 0.3, 512, dtype=np.float32))
    rows = np.arange(1024) % 512
    C["NDL"] = (-deltas[rows] / (L - 1)).reshape(8, 128).T
    C["IOTA"] = np.arange(512, dtype=np.float32)[None, :]
    C["IOTAB"] = (512.0 * np.arange(max(1, L // 512), dtype=np.float32))[None, :]
    return {k: np.ascontiguousarray(v, dtype=np.float32) for k, v in C.items()}


def hyena_inputs(nc, inp, depth, seqs):
    C = {}
    for s, L in seqs:
        for k, v in hy_consts(L).items():
            C[(s, k)] = inp(f"hc_{s}_{k}", list(v.shape))
    for n, sh in [("hyena_conv_w", [depth, 3, 1536]), ("hyena_conv_b", [depth, 1536]), ("filt_w1", [depth, 33, 64]),
                  ("filt_b1", [depth, 64]), ("filt_w2", [depth, 64, 64]), ("filt_b2", [depth, 64]), ("filt_w3", [depth, 64, 64]),
                  ("filt_b3", [depth, 64]), ("filt_w4", [depth, 64, 1024]), ("filt_freq", [depth, 64]), ("hyena_skip", [depth, 512])]:
        inp(n, sh)
    for s, L in seqs:
        C[(s, "HFD")] = dram(nc, "HFD_" + s, [1024, L], BF16)
        C[(s, "ZD")] = dram(nc, "ZD_" + s, [512, L], BF16)
        C[(s, "X0C")] = dram(nc, "X0C_" + s, [512, L], BF16)
    return C


def hyena_phase(nc, S, l, depth, seqs, ins, PROJH, MIX, ident, epsc, C):
    with ExitStack() as es:
        w1 = sb(nc, es, "hy_w1", [33, 64], F32)
        w2 = sb(nc, es, "hy_w2", [64, 64], F32)
        w3 = sb(nc, es, "hy_w3", [64, 64], F32)
        w4f = sb(nc, es, "hy_w4f", [64, 1024], F32)
        w4 = sb(nc, es, "hy_w4", [64, 1024], BF16)
        S.dma("sp", w1[:], ins["filt_w1"].t.ap()[l], reads=[ins["filt_w1"]], writes=[w1])
        S.dma("sp", w2[:], ins["filt_w2"].t.ap()[l], reads=[ins["filt_w2"]], writes=[w2])
        S.dma("sp", w3[:], ins["filt_w3"].t.ap()[l], reads=[ins["filt_w3"]], writes=[w3])
        S.dma("sp", w4f[:], ins["filt_w4"].t.ap()[l], reads=[ins["filt_w4"]], writes=[w4f])
        S.op("act", lambda e: e.copy(out=w4[:], in_=w4f[:]), reads=[w4f], writes=[w4])
        pv = sb(nc, es, "hy_pv", [64, 8], F32)
        S.dma("sp", pv[:, 0:1], ins["filt_freq"].t.ap()[l].rearrange("(p o) -> p o", o=1), reads=[ins["filt_freq"]], writes=[pv])
        for i_, nm in enumerate(("filt_b1", "filt_b2", "filt_b3")):
            S.dma("sp", pv[:, 1 + i_:2 + i_], ins[nm].t.ap()[l].rearrange("(p o) -> p o", o=1), reads=[ins[nm]], writes=[pv])
        S.op("dve", lambda e: e.tensor_scalar(out=pv[:, 4:5], in0=pv[:, 0:1], scalar1=1.0 / (2 * math.pi), scalar2=None, op0=ALU.mult),
             reads=[pv], writes=[pv])
        for i_ in range(3):
            S.op("dve", lambda e, i_=i_: e.tensor_tensor(out=pv[:, 5 + i_:6 + i_], in0=pv[:, 1 + i_:2 + i_], in1=pv[:, 4:5], op=ALU.mult),
                 reads=[pv], writes=[pv])
        skc = sb(nc, es, "hy_skc", [128, 4], F32)
        S.dma("sp", skc[:], ins["hyena_skip"].t.ap()[l].rearrange("(k p) -> p k", p=128), reads=[ins["hyena_skip"]], writes=[skc],
              allow_slow_non_contiguous=True)
        iota = sb(nc, es, "hy_iota", [128, 512], F32)
        fpool = Pool(nc, es, "hy_feat", 2, [33, 512], F32)
        hp = Pool(nc, es, "hy_h", 3, [64, 512], F32)
        up = Pool(nc, es, "hy_u", 2, [64, 512], F32)
        rp = Pool(nc, es, "hy_r", 2, [64, 512], F32)
        pm = Pool(nc, es, "hy_pm", 2, [64, 512], F32, space="psum")
        pf = Pool(nc, es, "hy_pf", 2, [128, 512], F32, space="psum")
        hfo = Pool(nc, es, "hy_hfo", 3, [128, 512], BF16)
        winj = sb(nc, es, "hy_winj", [128, 512], F32)
        for s, L in seqs:
            nblk = max(1, L // 512)
            bw = min(512, L)
            H3 = sb(nc, es, "hy_H3" + s, [64, L], BF16)
            ndl = sb(nc, es, "hy_ndl" + s, [128, 8], F32)
            iotab = sb(nc, es, "hy_iotab" + s, [128, nblk], F32)
            wblk = sb(nc, es, "hy_wblk" + s, [128, nblk], F32)
            S.dma("sp", ndl[:], C[(s, "NDL")].t.ap(), reads=[C[(s, "NDL")]], writes=[ndl])
            S.dma("sp", iota[:], bass.AP(C[(s, "IOTA")].t, 0, [[0, 128], [1, 512]]), reads=[C[(s, "IOTA")]], writes=[iota])
            S.dma("sp", iotab[:], bass.AP(C[(s, "IOTAB")].t, 0, [[0, 128], [1, nblk]]), reads=[C[(s, "IOTAB")]], writes=[iotab])
            for b in range(nblk):
                ft = fpool.next()
                S.dma("sp", ft[:, 0:bw], C[(s, "FEAT")].t.ap()[:, b * 512:b * 512 + bw], reads=[C[(s, "FEAT")]], writes=[ft])
                cur = ft
                for li, (w_, kdim) in enumerate(((w1, 33), (w2, 64), (w3, 64))):
                    ps = pm.next()
                    S.op("pe", lambda e, ps=ps, w_=w_, cur=cur, kdim=kdim, bw=bw: e.matmul(out=ps[:, 0:bw], lhsT=w_[0:kdim, :], rhs=cur[0:kdim, 0:bw],
                                                                                 start=True, stop=True), reads=[w_, cur], writes=[ps])
                    u = up.next()
                    r_ = rp.next()
                    S.op("dve", lambda e, ps=ps, u=u, li=li, bw=bw: e.tensor_scalar(out=u[:, 0:bw], in0=ps[:, 0:bw], scalar1=pv[:, 4:5], scalar2=pv[:, 5 + li:6 + li],
                                                                          op0=ALU.mult, op1=ALU.add), reads=[ps, pv], writes=[u])
                    S.op("dve", lambda e, u=u, r_=r_, bw=bw: e.tensor_scalar(out=r_[:, 0:bw], in0=u[:, 0:bw], scalar1=MAGIC, scalar2=MAGIC, op0=ALU.add, op1=ALU.subtract),
                         reads=[u], writes=[r_])
                    S.op("dve", lambda e, u=u, r_=r_, bw=bw: e.tensor_tensor(out=u[:, 0:bw], in0=u[:, 0:bw], in1=r_[:, 0:bw], op=ALU.subtract), reads=[u, r_], writes=[u])
                    if li < 2:
                        hh = hp.next()
                        S.op("act", lambda e, u=u, hh=hh, bw=bw: e.activation(out=hh[:, 0:bw], in_=u[:, 0:bw], func=AF.Sin, scale=TWO_PI), reads=[u], writes=[hh])
                        cur = hh
                    else:
                        S.op("act", lambda e, u=u, b=b, bw=bw, H3=H3: e.activation(out=H3[:, b * 512:b * 512 + bw], in_=u[:, 0:bw], func=AF.Sin, scale=TWO_PI),
                             reads=[u], writes=[H3])
            for rb in range(8):
                S.op("act", lambda e, rb=rb, ndl=ndl: e.activation(out=winj[:], in_=iota[:], func=AF.Exp, scale=ndl[:, rb:rb + 1]), reads=[iota, ndl], writes=[winj])
                S.op("act", lambda e, rb=rb, ndl=ndl, wblk=wblk, iotab=iotab: e.activation(out=wblk[:], in_=iotab[:], func=AF.Exp, scale=ndl[:, rb:rb + 1]), reads=[iotab, ndl], writes=[wblk])
                for b in range(nblk):
                    ps = pf.next()
                    S.op("pe", lambda e, ps=ps, rb=rb, b=b, bw=bw, H3=H3: e.matmul(out=ps[:, 0:bw], lhsT=w4[:, rb * 128:(rb + 1) * 128], rhs=H3[:, b * 512:b * 512 + bw],
                                                                  start=True, stop=True), reads=[w4, H3], writes=[ps])
                    o = hfo.next()
                    S.op("dve", lambda e, ps=ps, o=o, b=b, bw=bw, wblk=wblk: e.scalar_tensor_tensor(out=o[:, 0:bw], in0=ps[:, 0:bw], scalar=wblk[:, b:b + 1], in1=winj[:, 0:bw],
                                                                               op0=ALU.mult, op1=ALU.mult), reads=[ps, wblk, winj], writes=[o])
                    if b == 0:
                        if rb < 4:
                            S.op("dve", lambda e, o=o, rb=rb: e.tensor_tensor(out=o[:, 0:1], in0=o[:, 0:1], in1=skc[:, rb:rb + 1], op=ALU.add),
                                 reads=[o, skc], writes=[o])
                        else:
                            S.op("dve", lambda e, o=o: e.memset(o[:, 0:1], 0.0), reads=[o], writes=[o])
                    S.dma("pool", C[(s, "HFD")].t.ap()[rb * 128:(rb + 1) * 128, b * 512:b * 512 + bw], o[:, 0:bw], reads=[o], writes=[C[(s, "HFD")]])
        S.emit()

    with ExitStack() as es:
        hcw = sb(nc, es, "hy_cw", [128, 4, 12], F32)
        for tp in range(3):
            S.dma("sp", hcw[:, tp, :], ins["hyena_conv_w"].t.ap()[l, tp].rearrange("(k p) -> p k", p=128), reads=[ins["hyena_conv_w"]], writes=[hcw],
                  allow_slow_non_contiguous=True)
        S.dma("sp", hcw[:, 3, :], ins["hyena_conv_b"].t.ap()[l].rearrange("(k p) -> p k", p=128), reads=[ins["hyena_conv_b"]], writes=[hcw],
              allow_slow_non_contiguous=True)
        NPmax = 2048
        xin_p = Pool(nc, es, "hy_xin", 6, [128, NPmax + 2], BF16)
        acc_p = Pool(nc, es, "hy_acc", 6, [128, NPmax], F32)
        ob_p = Pool(nc, es, "hy_ob", 4, [128, NPmax], BF16)
        for s, L in seqs:
            NP = min(NPmax, L)
            for cb in range(4):
                for c0 in range(0, L, NP):
                    accs = []
                    xas = []
                    for a in range(3):
                        xa = xin_p.next()
                        row0 = a * 512 + cb * 128
                        S.dma("sp", xa[:, 0:NP + 2], PROJH[s].t.ap()[row0:row0 + 128, c0:c0 + NP + 2], reads=[PROJH[s]], writes=[xa])
                        xas.append(xa)
                        accs.append(acc_p.next())
                    for a in range(3):
                        kcol = a * 4 + cb
                        S.op("act", lambda e, xa=xas[a], acc=accs[a], kcol=kcol, NP=NP: e.activation(out=acc[:, 0:NP], in_=xa[:, 0:NP], func=AF.Identity,
                                                                                              scale=hcw[:, 0, kcol:kcol + 1], bias=hcw[:, 3, kcol:kcol + 1]),
                             reads=[xas[a], hcw], writes=[accs[a]])
                    for tp in (1, 2):
                        for a in range(3):
                            kcol = a * 4 + cb
                            S.op("dve", lambda e, xa=xas[a], acc=accs[a], kcol=kcol, tp=tp, NP=NP: e.scalar_tensor_tensor(
                                out=acc[:, 0:NP], in0=xa[:, tp:tp + NP], scalar=hcw[:, tp, kcol:kcol + 1], in1=acc[:, 0:NP], op0=ALU.mult, op1=ALU.add),
                                reads=[xas[a], hcw, accs[a]], writes=[accs[a]])
                    o0 = ob_p.next()
                    S.op("act", lambda e, o0=o0, a0=accs[0], NP=NP: e.copy(out=o0[:, 0:NP], in_=a0[:, 0:NP]), reads=[accs[0]], writes=[o0])
                    S.dma("pool", C[(s, "X0C")].t.ap()[cb * 128:(cb + 1) * 128, c0:c0 + NP], o0[:, 0:NP], reads=[o0], writes=[C[(s, "X0C")]])
                    oz = ob_p.next()
                    S.op("pool", lambda e, oz=oz, a1=accs[1], a2=accs[2], NP=NP: e.tensor_tensor(out=oz[:, 0:NP], in0=a1[:, 0:NP], in1=a2[:, 0:NP], op=ALU.mult),
                         reads=[accs[1], accs[2]], writes=[oz])
                    S.dma("pool", C[(s, "ZD")].t.ap()[cb * 128:(cb + 1) * 128, c0:c0 + NP], oz[:, 0:NP], reads=[oz], writes=[C[(s, "ZD")]])
        S.emit()

    for s, L in seqs:
        B = L // 128
        with ExitStack() as es:
            def cload(name, shape, dt):
                stg = sb(nc, es, "hy_cs_" + name, shape, F32)
                S.dma("sp", stg[:], C[(s, name)].t.ap(), reads=[C[(s, name)]], writes=[stg])
                if dt == F32:
                    return stg
                t = sb(nc, es, "hy_c_" + name, shape, BF16)
                S.op("act", lambda e: e.copy(out=t[:], in_=stg[:]), reads=[stg], writes=[t])
                return t
            F1CAT = cload("F1CAT", [128, 256], BF16)
            TWR = cload("TWR", [B, 128], F32)
            TWI = cload("TWI", [B, 128], F32)
            F2RE = cload("F2RE", [B, B], BF16)
            F2IM = cload("F2IM", [B, B], BF16)
            F2IMN = cload("F2IMN", [B, B], BF16)
            F2I_A = cload("F2I_A", [B, 2 * B], BF16)
            F2I_B = cload("F2I_B", [B, 2 * B], BF16)
            ITWR = cload("ITWR", [128, B], F32)
            ITWI = cload("ITWI", [128, B], F32)
            F1I_RE = cload("F1I_RE", [128, 128], BF16)
            F1I_IM = cload("F1I_IM", [128, 128], BF16)
            ZG = sb(nc, es, "hy_zg", [128, 64 * B], BF16)
            ATR = sb(nc, es, "hy_atr", [128, 64 * 128], BF16)
            ATI = sb(nc, es, "hy_ati", [128, 64 * 128], BF16)
            GR = sb(nc, es, "hy_gr", [128, 64 * 128], BF16)
            GI = sb(nc, es, "hy_gi", [128, 64 * 128], BF16)
            PR = sb(nc, es, "hy_pr", [128, 64 * 128], BF16)
            PI = sb(nc, es, "hy_pi", [128, 64 * 128], BF16)
            X0G = sb(nc, es, "hy_x0g", [128, 64 * B], BF16)
            U = sb(nc, es, "hy_uu", [128, 64 * B], BF16)
            TS = [[sb(nc, es, f"hy_t{q}{i}", [128, 512], F32) for i in range(4)] for q in range(2)]
            SSt = sb(nc, es, "hy_ss", [128, 2, B], F32)
            PS4s = [Buf(es.enter_context(nc.psum_tensor(_uname("hy_ps4"), [128, 1024], F32)), "ps4") for _ in range(2)]
            PSRs = [Buf(es.enter_context(nc.psum_tensor(_uname("hy_psr"), [128, 512], F32)), "psr") for _ in range(2)]
            PSIs = [Buf(es.enter_context(nc.psum_tensor(_uname("hy_psi"), [128, 512], F32)), "psi") for _ in range(2)]
            tctr = [0]

            def bcast_mid(t, np_, mid, inner):
                a = t[0:np_, 0:inner]
                return bass.AP(a.tensor, a.offset, [[a.ap[0][0], np_], [0, mid], [1, inner]])

            def cmul_evac(np_, re, im, cr, ci, out_re, out_im, mid, inner, rd, wr):
                T = TS[tctr[0] % 2]
                tctr[0] += 1
                n = mid * inner
                tv = [t[0:np_, 0:n].rearrange("p (c k) -> p c k", c=mid) for t in T]
                S.op("dve", lambda e: e.tensor_tensor(out=tv[0], in0=re, in1=cr, op=ALU.mult), reads=rd, writes=[T[0]])
                S.op("dve", lambda e: e.tensor_tensor(out=tv[1], in0=im, in1=ci, op=ALU.mult), reads=rd, writes=[T[1]])
                S.op("dve", lambda e: e.tensor_tensor(out=tv[2], in0=re, in1=ci, op=ALU.mult), reads=rd, writes=[T[2]])
                S.op("dve", lambda e: e.tensor_tensor(out=tv[3], in0=im, in1=cr, op=ALU.mult), reads=rd, writes=[T[3]])
                S.op("pool", lambda e: e.tensor_tensor(out=out_re, in0=tv[0], in1=tv[1], op=ALU.subtract), reads=[T[0], T[1]], writes=wr)
                S.op("pool", lambda e: e.tensor_tensor(out=out_im, in0=tv[2], in1=tv[3], op=ALU.add), reads=[T[2], T[3]], writes=wr)

            def fwd_fft(src_rows_ap, src_buf, mode):
                S.dma("sp", ZG[:, :].rearrange("p (c n) -> p c n", c=64), src_rows_ap.rearrange("c (n1 n2) -> n1 c n2", n2=B), reads=[src_buf], writes=[ZG])
                atr = ATR[0:B, :].rearrange("p (c k) -> p c k", c=64)
                ati = ATI[0:B, :].rearrange("p (c k) -> p c k", c=64)
                for cb in range(16):
                    PS4 = PS4s[cb % 2]
                    for ci in range(4):
                        c = cb * 4 + ci
                        S.op("pe", lambda e, c=c, ci=ci, PS4=PS4: e.matmul(out=PS4[0:B, ci * 256:(ci + 1) * 256], lhsT=ZG[:, c * B:(c + 1) * B], rhs=F1CAT[:],
                                                                          start=True, stop=True), reads=[ZG, F1CAT], writes=[PS4])
                    pv4 = PS4[0:B, :].rearrange("p (c t k) -> p c t k", c=4, t=2)
                    cmul_evac(B, pv4[:, :, 0, :], pv4[:, :, 1, :], bcast_mid(TWR, B, 4, 128), bcast_mid(TWI, B, 4, 128),
                              atr[:, cb * 4:(cb + 1) * 4, :], ati[:, cb * 4:(cb + 1) * 4, :], 4, 128, [PS4, TWR, TWI], [ATR, ATI])
                for cb in range(16):
                    PSR, PSI = PSRs[cb % 2], PSIs[cb % 2]
                    cols = slice(cb * 512, (cb + 1) * 512)
                    S.op("pe", lambda e, cols=cols, PSR=PSR: e.matmul(out=PSR[0:B, :], lhsT=F2RE[:], rhs=ATR[0:B, cols], start=True, stop=False),
                         reads=[F2RE, ATR], writes=[PSR])
                    S.op("pe", lambda e, cols=cols, PSR=PSR: e.matmul(out=PSR[0:B, :], lhsT=F2IMN[:], rhs=ATI[0:B, cols], start=False, stop=True),
                         reads=[F2IMN, ATI], writes=[PSR])
                    S.op("pe", lambda e, cols=cols, PSI=PSI: e.matmul(out=PSI[0:B, :], lhsT=F2IM[:], rhs=ATR[0:B, cols], start=True, stop=False),
                         reads=[F2IM, ATR], writes=[PSI])
                    S.op("pe", lambda e, cols=cols, PSI=PSI: e.matmul(out=PSI[0:B, :], lhsT=F2RE[:], rhs=ATI[0:B, cols], start=False, stop=True),
                         reads=[F2RE, ATI], writes=[PSI])
                    gs = cols
                    if mode == "set":
                        S.op("act", lambda e, gs=gs, PSR=PSR: e.copy(out=GR[0:B, gs], in_=PSR[0:B, :]), reads=[PSR], writes=[GR])
                        S.op("act", lambda e, gs=gs, PSI=PSI: e.copy(out=GI[0:B, gs], in_=PSI[0:B, :]), reads=[PSI], writes=[GI])
                    elif mode == "accconj":
                        S.op("dve", lambda e, gs=gs, PSR=PSR: e.tensor_tensor(out=GR[0:B, gs], in0=PSR[0:B, :], in1=GR[0:B, gs], op=ALU.add), reads=[PSR, GR], writes=[GR])
                        S.op("dve", lambda e, gs=gs, PSI=PSI: e.tensor_tensor(out=GI[0:B, gs], in0=GI[0:B, gs], in1=PSI[0:B, :], op=ALU.subtract), reads=[PSI, GI], writes=[GI])
                    else:
                        v = lambda t: t[0:B, gs].rearrange("p (c k) -> p c k", c=4)
                        vp = lambda t: t[0:B, :].rearrange("p (c k) -> p c k", c=4)
                        cmul_evac(B, vp(PSR), vp(PSI), v(GR), v(GI), v(PR), v(PI), 4, 128, [PSR, PSI, GR, GI], [PR, PI])

            for g in range(8):
                fwd_fft(C[(s, "HFD")].t.ap()[g * 64:(g + 1) * 64, :], C[(s, "HFD")], "set")
                fwd_fft(C[(s, "HFD")].t.ap()[512 + g * 64:512 + (g + 1) * 64, :], C[(s, "HFD")], "accconj")
                S.dma("pool", X0G[:, :].rearrange("p (c n) -> p c n", c=64), C[(s, "X0C")].t.ap()[g * 64:(g + 1) * 64, :].rearrange("c (n1 n2) -> n1 c n2", n2=B),
                      reads=[C[(s, "X0C")]], writes=[X0G])
                fwd_fft(C[(s, "ZD")].t.ap()[g * 64:(g + 1) * 64, :], C[(s, "ZD")], "mul")
                nb2 = 1024 // (2 * B)
                for cb in range(64 // nb2):
                    PS4 = PS4s[cb % 2]
                    for ci in range(nb2):
                        c = cb * nb2 + ci
                        S.op("pe", lambda e, c=c, ci=ci, PS4=PS4: e.matmul(out=PS4[:, ci * 2 * B:(ci + 1) * 2 * B], lhsT=PR[0:B, c * 128:(c + 1) * 128], rhs=F2I_A[:],
                                                                          start=True, stop=False), reads=[PR, F2I_A], writes=[PS4])
                        S.op("pe", lambda e, c=c, ci=ci, PS4=PS4: e.matmul(out=PS4[:, ci * 2 * B:(ci + 1) * 2 * B], lhsT=PI[0:B, c * 128:(c + 1) * 128], rhs=F2I_B[:],
                                                                          start=False, stop=True), reads=[PI, F2I_B], writes=[PS4])
                    pv4 = PS4[:, :].rearrange("p (c t n) -> p c t n", c=nb2, t=2)
                    dtr = ATR[:, 0:64 * B].rearrange("p (c n) -> p c n", c=64)
                    dti = ATI[:, 0:64 * B].rearrange("p (c n) -> p c n", c=64)
                    cmul_evac(128, pv4[:, :, 0, :], pv4[:, :, 1, :], bcast_mid(ITWR, 128, nb2, B), bcast_mid(ITWI, 128, nb2, B),
                              dtr[:, cb * nb2:(cb + 1) * nb2, :], dti[:, cb * nb2:(cb + 1) * nb2, :], nb2, B, [PS4, ITWR, ITWI], [ATR, ATI])
                ncol = 64 * B
                pxs = [PSRs[0], PSIs[0], PSRs[1], PSIs[1]]
                for qi, q0 in enumerate(range(0, ncol, 512)):
                    PX = pxs[qi % 4]
                    cols = slice(q0, q0 + 512)
                    S.op("pe", lambda e, PX=PX, cols=cols: e.matmul(out=PX[:, :], lhsT=F1I_RE[:], rhs=ATR[:, cols], start=True, stop=False),
                         reads=[F1I_RE, ATR], writes=[PX])
                    S.op("pe", lambda e, PX=PX, cols=cols: e.matmul(out=PX[:, :], lhsT=F1I_IM[:], rhs=ATI[:, cols], start=False, stop=True),
                         reads=[F1I_IM, ATI], writes=[PX])
                    S.op("dve", lambda e, PX=PX, cols=cols: e.tensor_tensor(out=U[:, cols], in0=PX[:, :], in1=X0G[:, cols], op=ALU.mult),
                         reads=[PX, X0G], writes=[U])
                S.op("act", lambda e: e.activation(out=PR[:, 0:ncol], in_=U[:, 0:ncol], func=AF.Square), reads=[U], writes=[PR])
                S.op("dve", lambda e: e.tensor_reduce(out=SSt[:, 0, :], in_=PR[:, 0:ncol].rearrange("p (c n) -> p n c", c=64), axis=AX.X, op=ALU.add),
                     reads=[PR], writes=[SSt])
                S.op("act", lambda e: e.activation(out=SSt[:, 1, :], in_=SSt[:, 0, :], func=AF.Sqrt, scale=1.0 / 64, bias=epsc[:]), reads=[SSt, epsc], writes=[SSt])
                S.op("dve", lambda e: e.reciprocal(out=SSt[:, 1, :], in_=SSt[:, 1, :]), reads=[SSt], writes=[SSt])
                rsb = bass.AP(SSt.t, SSt[:, 1, :].offset, [[SSt[:].ap[0][0], 128], [0, 64], [1, B]])
                S.op("dve", lambda e, rsb=rsb: e.tensor_tensor(out=ZG[:, :].rearrange("p (c n) -> p c n", c=64), in0=U[:, :].rearrange("p (c n) -> p c n", c=64),
                                                            in1=rsb, op=ALU.mult), reads=[U, SSt], writes=[ZG])
                S.dma("pool", MIX[s].t.ap()[g * 64:(g + 1) * 64, :].rearrange("c (n1 n2) -> n1 c n2", n2=B), ZG[:, :].rearrange("p (c n) -> p c n", c=64),
                      reads=[ZG], writes=[MIX[s]])
            S.emit()


MIXERS = True


def kernel(**inputs):
    from concourse.bass_utils import run_bass_kernel_spmd
    nc, names = build(LP_FULL, LS_FULL, DEPTH, mixers=MIXERS)
    in_maps = [make_in_map(inputs, names, r, LP_FULL, LS_FULL, DEPTH) for r in range(4)]
    res = run_bass_kernel_spmd(nc, in_maps, core_ids=list(range(4)))
    y_prompt = np.asarray(res.results[0]["y_prompt"], dtype=np.float32)[None]
    y_sample = np.stack([np.asarray(res.results[r]["y_sample"], dtype=np.float32) for r in range(4)], axis=0)
    return (y_prompt, y_sample)
```

```python
import math
import numpy as np
from contextlib import ExitStack
import concourse.bass as bass
import concourse.mybir as mybir

F32 = mybir.dt.float32
BF16 = mybir.dt.bfloat16
AF = mybir.ActivationFunctionType
ALU = mybir.AluOpType
AX = mybir.AxisListType

SAME_ENG_SYNC = {"pool", "dve", "act"}


class Buf:
    __slots__ = ("t", "name", "last_w", "readers")

    def __init__(self, t, name):
        self.t = t
        self.name = name
        self.last_w = None
        self.readers = []

    def __getitem__(self, idx):
        return self.t[idx]

    def ap(self):
        return self.t.ap() if hasattr(self.t, "ap") and callable(getattr(self.t, "ap")) else self.t[:]


class Sched:
    NDSEM = 12

    def __init__(self, nc, es):
        self.nc = nc
        self.es = es
        self.eng = {"pe": nc.tensor, "act": nc.scalar, "dve": nc.vector, "pool": nc.gpsimd, "sp": nc.sync}
        self.prog = {e: [] for e in self.eng}
        self.csem = {e: es.enter_context(nc.semaphore("c_" + e)) for e in ("pe", "act", "dve", "pool")}
        self.ccnt = {e: 0 for e in self.csem}
        self.dsem = {q: [es.enter_context(nc.semaphore(f"d_{q}{i}")) for i in range(self.NDSEM)]
                     for q in ("sp", "pool", "act")}
        self.dcnt = {q: [0] * self.NDSEM for q in self.dsem}
        self.dnext = {q: 0 for q in self.dsem}
        self.seen = {e: {} for e in self.eng}
        self.ninst = 0
        self.final_events = []

    def _sem_of(self, key):
        if key[0] == "c":
            return self.csem[key[1]]
        return self.dsem[key[1]][key[2]]

    def _deps(self, engine, reads, writes):
        deps = {}
        def add(ev):
            if ev is None:
                return
            k, v = ev
            if deps.get(k, 0) < v:
                deps[k] = v
        for b in reads:
            add(b.last_w)
        for b in writes:
            add(b.last_w)
            for ev in b.readers:
                add(ev)
        waits = []
        for k, v in deps.items():
            if k == ("c", engine) and (engine == "pe" or engine not in SAME_ENG_SYNC):
                continue
            if self.seen[engine].get(k, 0) < v:
                self.seen[engine][k] = v
                waits.append((k, v))
        return waits

    def _record(self, ev, reads, writes):
        for b in writes:
            b.last_w = ev
            b.readers = []
        for b in reads:
            b.readers.append(ev)

    def op(self, engine, fn, reads=(), writes=()):
        waits = self._deps(engine, reads, writes)
        self.ccnt[engine] += 1
        ev = (("c", engine), self.ccnt[engine])
        self._record(ev, reads, writes)
        self.prog[engine].append((waits, fn, (self.csem[engine], 1)))
        self.ninst += 1
        return ev

    def dma(self, queue, out, in_, reads=(), writes=(), **kw):
        waits = self._deps(queue, reads, writes)
        i = self.dnext[queue]
        self.dnext[queue] = (i + 1) % self.NDSEM
        prev = self.dcnt[queue][i]
        key = ("d", queue, i)
        if prev > 0 and self.seen[queue].get(key, 0) < prev:
            self.seen[queue][key] = prev
            waits.append((key, prev))
        self.dcnt[queue][i] = prev + 16
        ev = (key, prev + 16)
        self._record(ev, reads, writes)
        fn = lambda e, out=out, in_=in_, kw=kw: e.dma_start(out=out, in_=in_, **kw)
        self.prog[queue].append((waits, fn, (self.dsem[queue][i], 16)))
        self.ninst += 1
        return ev

    def emit(self, final_bufs=()):
        if not hasattr(self, "ecount"):
            self.ecount = {e: 0 for e in self.csem}
            self.emap = {e: {} for e in self.csem}
        needed = set()
        for engine in self.prog:
            for waits, fn, inc in self.prog[engine]:
                for k, v in waits:
                    if k[0] == "c":
                        needed.add((k[1], v))
        logical = {}
        base = {e: self.ccnt[e] - sum(1 for w, f, inc in self.prog[e] if inc[0] is self.csem.get(e)) for e in self.csem}
        for engine in self.csem:
            c = base[engine]
            ids = []
            for waits, fn, inc in self.prog[engine]:
                if inc[0] is self.csem[engine]:
                    c += 1
                    ids.append(c)
                else:
                    ids.append(None)
            logical[engine] = ids
            last = [i for i in ids if i is not None]
            if last:
                needed.add((engine, last[-1]))
            for i in ids:
                if i is not None and (engine, i) in needed:
                    self.ecount[engine] += 1
                    self.emap[engine][i] = self.ecount[engine]
        fin = {}
        for e2 in self.csem:
            if self.ccnt[e2] > 0:
                fin[("c", e2)] = self.ccnt[e2]
        for q in self.dsem:
            for i in range(self.NDSEM):
                if self.dcnt[q][i] > 0:
                    fin[("d", q, i)] = self.dcnt[q][i]

        def semval(k, v):
            if k[0] == "c":
                m = self.emap[k[1]]
                if v in m:
                    return m[v]
                cands = [lv for lv in m if lv >= v]
                return m[min(cands)]
            return v

        with self.nc.Block() as block:
            def mk(engine):
                def body(e):
                    ids = logical.get(engine)
                    for idx, (waits, fn, (sem, n)) in enumerate(self.prog[engine]):
                        for k, v in waits:
                            e.wait_ge(self._sem_of(k), semval(k, v))
                        inst = fn(e)
                        if ids is not None and ids[idx] is not None:
                            if (engine, ids[idx]) in needed:
                                inst.then_inc(sem, n)
                        else:
                            inst.then_inc(sem, n)
                    for k, v in fin.items():
                        if k == ("c", engine):
                            continue
                        if self.seen[engine].get(k, 0) < v:
                            self.seen[engine][k] = v
                            e.wait_ge(self._sem_of(k), semval(k, v))
                return body
            block.sync(mk("sp"))
            block.tensor(mk("pe"))
            block.scalar(mk("act"))
            block.vector(mk("dve"))
            block.gpsimd(mk("pool"))
        self.prog = {e: [] for e in self.eng}


_UNIQ = [0]


def _uname(name):
    _UNIQ[0] += 1
    return f"{name}_{_UNIQ[0]}"


class Pool:
    def __init__(self, nc, es, name, n, shape, dtype, space="sbuf"):
        self.bufs = []
        name = _uname(name)
        for i in range(n):
            if space == "sbuf":
                t = es.enter_context(nc.sbuf_tensor(f"{name}{i}", list(shape), dtype))
            else:
                t = es.enter_context(nc.psum_tensor(f"{name}{i}", list(shape), dtype))
            self.bufs.append(Buf(t, f"{name}{i}"))
        self.i = 0

    def next(self):
        b = self.bufs[self.i]
        self.i = (self.i + 1) % len(self.bufs)
        return b


class ViewPool:
    def __init__(self, nc, es, name, n, w, dtype):
        per = (512 if dtype == F32 else 1024) // w
        self.bufs = []
        nb = (n + per - 1) // per
        for b in range(nb):
            t = es.enter_context(nc.psum_tensor(_uname(name), [128, per * w], dtype))
            for i in range(per):
                if len(self.bufs) < n:
                    self.bufs.append(Buf(t[:, i * w:(i + 1) * w], f"{name}{b}_{i}"))
        self.i = 0

    def next(self):
        b = self.bufs[self.i]
        self.i = (self.i + 1) % len(self.bufs)
        return b


def sb(nc, es, name, shape, dtype):
    name = _uname(name)
    return Buf(es.enter_context(nc.sbuf_tensor(name, list(shape), dtype)), name)


def dram(nc, name, shape, dtype, kind="Internal"):
    return Buf(nc.dram_tensor(name, list(shape), dtype, kind=kind), name)


D = 1024
DIN = 4096
DFF = 4096
DEPTH = 4
LP_FULL = 16384
LS_FULL = 4096
EPS = 1e-6
WIN = 510


def _bc(buf_ap, nparts, mid, inner):
    pstep = buf_ap.ap[0][0]
    return bass.AP(buf_ap.tensor, buf_ap.offset, [[pstep, nparts], [0, mid], [1, inner]])


class K:
    pass


def build(LP, LS, depth, mixers=True):
    nc = bass.Bass("TRN2", target_bir_lowering=False)
    ins = {}
    def inp(name, shape):
        ins[name] = dram(nc, name, shape, F32, "ExternalInput")
        return ins[name]
    seqs = [("p", LP), ("s", LS)]
    xin = {"p": inp("x_prompt", [LP, D]), "s": inp("x_sample", [LS, D])}
    for n, sh in [("norm_mix_pre", [depth, D]), ("norm_mix_post", [depth, D]), ("norm_ffn_pre", [depth, D]),
                  ("norm_ffn_post", [depth, D]), ("w_in", [depth, D, DIN]), ("w_out", [depth, D, D]),
                  ("ffn_w_up", [depth, D, 2 * DFF]), ("ffn_conv_w", [depth, 3, 2 * DFF]),
                  ("ffn_conv_b", [depth, 2 * DFF]), ("ffn_w_down", [depth, DFF, D]),
                  ("hyena_out_norm", [depth, 512]), ("hgrn_out_norm", [depth, 512])]:
        inp(n, sh)
    yout = {"p": dram(nc, "y_prompt", [LP, D], F32, "ExternalOutput"),
            "s": dram(nc, "y_sample", [LS, D], F32, "ExternalOutput")}
    XA = {s: dram(nc, "XA_" + s, [L + 2, D], F32) for s, L in seqs}
    XB = {s: dram(nc, "XB_" + s, [L, D], F32) for s, L in seqs}
    MIX = {s: dram(nc, "MIX_" + s, [D, L], BF16) for s, L in seqs}
    PROJH = {s: dram(nc, "PROJH_" + s, [1536, L + 2], BF16) for s, L in seqs}
    PROJG = {s: dram(nc, "PROJG_" + s, [2560, L], F32) for s, L in seqs}
    WUP = dram(nc, "WUP", [32, 128, 2048], BF16)
    OFD = {s: dram(nc, "OFD_" + s, [512, L], F32) for s, L in seqs}
    inp("hgrn_lower_bounds", [2, DEPTH, 512])
    C = hyena_inputs(nc, inp, depth, seqs)

    with ExitStack() as es0:
        S = Sched(nc, es0)
        ident = sb(nc, es0, "ident", [128, 128], BF16)
        zrow = sb(nc, es0, "zrow", [128, D], F32)
        with ExitStack() as es:
            idf = sb(nc, es, "idf", [128, 128], F32)
            S.op("pool", lambda e: e.memset(idf[:], 0.0), writes=[idf])
            S.op("pool", lambda e: e.affine_select(out=idf[:], in_=idf[:], pattern=[[-1, 128]], compare_op=ALU.not_equal,
                                                   fill=1.0, base=0, channel_multiplier=1), reads=[idf], writes=[idf])
            S.op("act", lambda e: e.copy(out=ident[:], in_=idf[:]), reads=[idf], writes=[ident])
            S.op("pool", lambda e: e.memset(zrow[:], 0.0), writes=[zrow])
            for s, L in seqs:
                S.dma("sp", XA[s].t.ap()[0:1, :], zrow[0:1, :], reads=[zrow], writes=[XA[s]])
                S.dma("sp", XA[s].t.ap()[L + 1:L + 2, :], zrow[0:1, :], reads=[zrow], writes=[XA[s]])
                zc = sb(nc, es, "zc" + s, [128, 2], BF16)
                S.op("pool", lambda e, zc=zc: e.memset(zc[:], 0.0), writes=[zc])
                for r0 in range(0, 1536, 128):
                    S.dma("sp", PROJH[s].t.ap()[r0:r0 + 128, 0:1], zc[:, 0:1], reads=[zc], writes=[PROJH[s]], allow_slow_non_contiguous=True)
                    S.dma("sp", PROJH[s].t.ap()[r0:r0 + 128, L + 1:L + 2], zc[:, 1:2], reads=[zc], writes=[PROJH[s]], allow_slow_non_contiguous=True)
                if not mixers:
                    zb = sb(nc, es, "zb" + s, [128, 2048], BF16)
                    S.op("pool", lambda e, zb=zb: e.memset(zb[:], 0.0), writes=[zb])
                    for r0 in range(0, D, 128):
                        for c0 in range(0, L, 2048):
                            cwid = min(2048, L - c0)
                            S.dma("sp", MIX[s].t.ap()[r0:r0 + 128, c0:c0 + cwid], zb[:, 0:cwid], reads=[zb], writes=[MIX[s]])
            S.emit()

        def colvec(es, name, src_ap_1d, n):
            t = sb(nc, es, name, [128, n], F32)
            return t, src_ap_1d.rearrange("(k p) -> p k", p=128)

        def norm_window(es, pools, src_rows_ap, nrows, hT, tagbufs):
            nsub = (nrows + 127) // 128
            for j in range(nsub):
                r = min(128, nrows - j * 128)
                xt = pools["xt"].next()
                S.dma("sp", xt[0:r, :], src_rows_ap[j * 128:j * 128 + r, :], reads=tagbufs, writes=[xt])
                junk = pools["junk"].next()
                st = pools["st"].next()
                S.op("act", lambda e, xt=xt, junk=junk, st=st, r=r: e.activation(out=junk[0:r, :], in_=xt[0:r, :], func=AF.Square,
                                                                             accum_out=st[0:r, 0:1]), reads=[xt], writes=[junk, st])
                S.op("act", lambda e, st=st, r=r: e.activation(out=st[0:r, 1:2], in_=st[0:r, 0:1], func=AF.Sqrt,
                                                               scale=1.0 / D, bias=epsc[0:r, :]), reads=[st, epsc], writes=[st])
                S.op("dve", lambda e, st=st, r=r: e.reciprocal(out=st[0:r, 2:3], in_=st[0:r, 1:2]), reads=[st], writes=[st])
                hb = pools["hb"].next()
                S.op("act", lambda e, hb=hb, xt=xt, st=st, r=r: e.activation(out=hb[0:r, :], in_=xt[0:r, :], func=AF.Copy,
                                                                            scale=st[0:r, 2:3]), reads=[xt, st], writes=[hb])
                for kq in range(2):
                    pt = pools["ptr"].next()
                    for kk_ in range(4):
                        k = kq * 4 + kk_
                        S.op("pe", lambda e, pt=pt, hb=hb, k=k, kk_=kk_, r=r: e.transpose(out=pt[:, kk_ * 128:kk_ * 128 + r], in_=hb[0:r, k * 128:(k + 1) * 128],
                                                                                      identity=ident[0:r, 0:r]), reads=[hb, ident], writes=[pt])
                    eng = "dve" if kq == 0 else "act"
                    src = pt[:, :].rearrange("p (a b) -> p a b", a=4)[:, :, 0:r]
                    dst_ = hT[:, kq * 4:(kq + 1) * 4, j * 128:j * 128 + r]
                    if eng == "dve":
                        S.op("dve", lambda e, src=src, dst_=dst_: e.tensor_copy(out=dst_, in_=src), reads=[pt], writes=[hT])
                    else:
                        S.op("act", lambda e, src=src, dst_=dst_: e.copy(out=dst_, in_=src), reads=[pt], writes=[hT])

        epsc = sb(nc, es0, "epsc", [128, 1], F32)
        S.op("pool", lambda e: e.memset(epsc[:], EPS), writes=[epsc])
        onec = sb(nc, es0, "onec", [128, 1], F32)
        S.op("pool", lambda e: e.memset(onec[:], 1.0), writes=[onec])

        for l in range(depth):
            with ExitStack() as es:
                WIN_SB = sb(nc, es, "win_sb", [128, 8, DIN], BF16)
                gcol, gsrc = colvec(es, "gcolA", ins["norm_mix_pre"].t.ap()[l], 8)
                S.dma("sp", gcol[:], gsrc, reads=[ins["norm_mix_pre"]], writes=[gcol], allow_slow_non_contiguous=True)
                stg = Pool(nc, es, "stgA", 2, [128, 2048], F32)
                for k in range(8):
                    for h in range(2):
                        st_ = stg.next()
                        S.dma("sp" if h == 0 else "pool", st_[:], ins["w_in"].t.ap()[l, k * 128:(k + 1) * 128, h * 2048:(h + 1) * 2048],
                              reads=[ins["w_in"]], writes=[st_])
                        S.op("act" if h == 0 else "dve",
                             (lambda e, st_=st_, k=k, h=h: e.activation(out=WIN_SB[:, k, h * 2048:(h + 1) * 2048], in_=st_[:], func=AF.Copy,
                                                                      scale=gcol[:, k:k + 1])) if h == 0 else
                             (lambda e, st_=st_, k=k, h=h: e.tensor_scalar(out=WIN_SB[:, k, h * 2048:(h + 1) * 2048], in0=st_[:],
                                                                         scalar1=gcol[:, k:k + 1], scalar2=None, op0=ALU.mult)),
                             reads=[st_, gcol], writes=[WIN_SB])
                pools = {"xt": Pool(nc, es, "xtA", 2, [128, D], F32), "junk": Pool(nc, es, "junkA", 1, [128, D], F32),
                         "st": Pool(nc, es, "stA", 4, [128, 4], F32), "hb": Pool(nc, es, "hbA", 2, [128, D], BF16),
                         "ptr": Pool(nc, es, "ptrA", 2, [128, 512], BF16, space="psum")}
                hTp = Pool(nc, es, "hTA", 2, [128, 8, 512], BF16)
                psA = Pool(nc, es, "psA", 4, [128, 512], F32, space="psum")
                oh = Pool(nc, es, "ohA", 3, [128, 512], BF16)
                og = Pool(nc, es, "ogA", 3, [128, 512], F32)
                for s, L in seqs:
                    src = xin[s] if l == 0 else XB[s]
                    for t0 in range(0, L, 512):
                        n = min(512, L - t0)
                        hT = hTp.next()
                        norm_window(es, pools, src.t.ap()[t0:t0 + n, :], n, hT, [src])
                        if not mixers:
                            continue
                        for m in range(32):
                            ps = psA.next()
                            for k in range(8):
                                S.op("pe", lambda e, ps=ps, k=k, m=m, hT=hT, n=n: e.matmul(out=ps[:, 0:n], lhsT=WIN_SB[:, k, m * 128:(m + 1) * 128],
                                                                                          rhs=hT[:, k, 0:n], start=(k == 0), stop=(k == 7)),
                                     reads=[WIN_SB, hT], writes=[ps])
                            if m < 12:
                                o = oh.next()
                                S.op("act" if m % 2 else "dve", (lambda e, o=o, ps=ps, n=n: e.copy(out=o[:, 0:n], in_=ps[:, 0:n])) if m % 2 else
                                     (lambda e, o=o, ps=ps, n=n: e.tensor_copy(out=o[:, 0:n], in_=ps[:, 0:n])), reads=[ps], writes=[o])
                                S.dma("pool", PROJH[s].t.ap()[m * 128:(m + 1) * 128, 1 + t0:1 + t0 + n], o[:, 0:n], reads=[o], writes=[PROJH[s]])
                            else:
                                o = og.next()
                                S.op("act" if m % 2 else "dve", (lambda e, o=o, ps=ps, n=n: e.copy(out=o[:, 0:n], in_=ps[:, 0:n])) if m % 2 else
                                     (lambda e, o=o, ps=ps, n=n: e.tensor_copy(out=o[:, 0:n], in_=ps[:, 0:n])), reads=[ps], writes=[o])
                                S.dma("pool", PROJG[s].t.ap()[(m - 12) * 128:(m - 11) * 128, t0:t0 + n], o[:, 0:n], reads=[o], writes=[PROJG[s]])
                S.emit()

            if mixers:
                hgrn_phase(nc, S, l, depth, seqs, ins, PROJG, MIX, OFD, ident, epsc)
                hyena_phase(nc, S, l, depth, seqs, ins, PROJH, MIX, ident, epsc, C)

            with ExitStack() as es:
                WO = sb(nc, es, "wo_sb", [128, 8, D], BF16)
                WDN = sb(nc, es, "wdn_sb", [128, 32, D], BF16)
                gco = sb(nc, es, "gco", [128, 8], F32)
                S.dma("sp", gco[:, 0:4], ins["hyena_out_norm"].t.ap()[l].rearrange("(k p) -> p k", p=128), reads=[ins["hyena_out_norm"]],
                      writes=[gco], allow_slow_non_contiguous=True)
                S.dma("sp", gco[:, 4:8], ins["hgrn_out_norm"].t.ap()[l].rearrange("(k p) -> p k", p=128), reads=[ins["hgrn_out_norm"]],
                      writes=[gco], allow_slow_non_contiguous=True)
                gcf, gsrc = colvec(es, "gcf", ins["norm_ffn_pre"].t.ap()[l], 8)
                S.dma("sp", gcf[:], gsrc, reads=[ins["norm_ffn_pre"]], writes=[gcf], allow_slow_non_contiguous=True)
                cw = sb(nc, es, "cw", [128, 4, 64], F32)
                for tp in range(3):
                    S.dma("sp", cw[:, tp, :], ins["ffn_conv_w"].t.ap()[l, tp].rearrange("(k p) -> p k", p=128), reads=[ins["ffn_conv_w"]],
                          writes=[cw], allow_slow_non_contiguous=True)
                S.dma("sp", cw[:, 3, :], ins["ffn_conv_b"].t.ap()[l].rearrange("(k p) -> p k", p=128), reads=[ins["ffn_conv_b"]],
                      writes=[cw], allow_slow_non_contiguous=True)
                GPM = sb(nc, es, "gpm", [128, D], F32)
                GPF = sb(nc, es, "gpf", [128, D], F32)
                S.dma("sp", GPM[:], bass.AP(ins["norm_mix_post"].t, l * D, [[0, 128], [1, D]]), reads=[ins["norm_mix_post"]], writes=[GPM])
                S.dma("sp", GPF[:], bass.AP(ins["norm_ffn_post"].t, l * D, [[0, 128], [1, D]]), reads=[ins["norm_ffn_post"]], writes=[GPF])
                pools = {"xt": Pool(nc, es, "xtC", 2, [128, D], F32), "junk": Pool(nc, es, "junkC", 1, [128, D], F32),
                         "st": Pool(nc, es, "stC", 4, [128, 4], F32), "hb": Pool(nc, es, "hbC", 2, [128, D], BF16),
                         "ptr": Pool(nc, es, "ptrC", 2, [128, 512], BF16, space="psum")}
                psO = Pool(nc, es, "psO", 1, [128, D], F32, space="psum")
                tmpf = Pool(nc, es, "tmpf", 2, [128, D], F32)
                esC0 = ExitStack()
                stg = Pool(nc, esC0, "stgC", 2, [128, 2048], F32)
                for k in range(8):
                    st_ = stg.next()
                    S.dma("sp", st_[:, 0:D], ins["w_out"].t.ap()[l, k * 128:(k + 1) * 128, :], reads=[ins["w_out"]], writes=[st_])
                    S.op("act", lambda e, st_=st_, k=k: e.activation(out=WO[:, k, :], in_=st_[:, 0:D], func=AF.Copy, scale=gco[:, k:k + 1]),
                         reads=[st_, gco], writes=[WO])
                for k in range(0, 32, 2):
                    st_ = stg.next()
                    S.dma("sp", st_[:].rearrange("p (a b) -> p a b", a=2),
                          ins["ffn_w_down"].t.ap()[l, k * 128:(k + 2) * 128, :].rearrange("(a p) n -> p a n", p=128),
                          reads=[ins["ffn_w_down"]], writes=[st_])
                    S.op("dve", lambda e, st_=st_, k=k: e.tensor_copy(out=WDN[:, k:k + 2, :].rearrange("p a b -> p (a b)"), in_=st_[:]),
                         reads=[st_], writes=[WDN])
                wst = Pool(nc, esC0, "wstC", 2, [128, 2048], BF16)
                for k in range(8):
                    for h in range(4):
                        st_ = stg.next()
                        S.dma("sp", st_[:], ins["ffn_w_up"].t.ap()[l, k * 128:(k + 1) * 128, h * 2048:(h + 1) * 2048], reads=[ins["ffn_w_up"]], writes=[st_])
                        wb = wst.next()
                        S.op("act" if h % 2 else "dve",
                             (lambda e, st_=st_, wb=wb, k=k: e.activation(out=wb[:], in_=st_[:], func=AF.Copy, scale=gcf[:, k:k + 1])) if h % 2 else
                             (lambda e, st_=st_, wb=wb, k=k: e.tensor_scalar(out=wb[:], in0=st_[:], scalar1=gcf[:, k:k + 1], scalar2=None, op0=ALU.mult)),
                             reads=[st_, gcf], writes=[wb])
                        j0_, half_ = (h % 2) * 16, h // 2
                        S.dma("pool", WUP.t.ap()[j0_:j0_ + 16, :, k * 256 + half_ * 128:k * 256 + half_ * 128 + 128].rearrange("j p c -> p j c"),
                              wb[:].rearrange("p (j c) -> p j c", j=16), reads=[wb], writes=[WUP])
                S.emit()
                esC0.close()
                esC1 = ExitStack()
                mixT = Pool(nc, esC1, "mixT", 2, [128, 8, 512], BF16)

                for s, L in seqs:
                    src = xin[s] if l == 0 else XB[s]
                    for t0 in range(0, L, 512):
                        n = min(512, L - t0)
                        mt = mixT.next()
                        S.dma("sp", mt[:, :, 0:n], MIX[s].t.ap()[:, t0:t0 + n].rearrange("(k p) n -> p k n", p=128), reads=[MIX[s]], writes=[mt])
                        for j in range((n + 127) // 128):
                            r = min(128, n - j * 128)
                            ps = psO.next()
                            for nb in range(2):
                                for k in range(8):
                                    S.op("pe", lambda e, ps=ps, mt=mt, k=k, nb=nb, j=j, r=r: e.matmul(
                                        out=ps[0:r, nb * 512:(nb + 1) * 512], lhsT=mt[:, k, j * 128:j * 128 + r], rhs=WO[:, k, nb * 512:(nb + 1) * 512],
                                        start=(k == 0), stop=(k == 7)), reads=[mt, WO], writes=[ps])
                            residual_update(S, pools, tmpf, ps, r, GPM, src.t.ap()[t0 + j * 128:t0 + j * 128 + r, :], src,
                                            XA[s].t.ap()[1 + t0 + j * 128:1 + t0 + j * 128 + r, :], XA[s], epsc)
                S.emit()
                esC1.close()
                hTp = Pool(nc, es, "hTC", 2, [128, 8, 512], BF16)
                G = sb(nc, es, "Gffn", [128, 32, 512], BF16)
                wup = Pool(nc, es, "wupC", 3, [128, 8, 256], BF16)
                psU = Pool(nc, es, "psU", 4, [128, 512], F32, space="psum")
                cva = Pool(nc, es, "cva2", 4, [128, 512], F32)
                cvb = Pool(nc, es, "cvb2", 4, [128, 512], F32)
                gt = Pool(nc, es, "gt2", 3, [128, 512], F32)
                wins = []
                for s, L in seqs:
                    dst = yout[s] if l == depth - 1 else XB[s]
                    for p0 in range(0, L, WIN):
                        wins.append((s, L, dst, p0, min(WIN, L - p0)))
                hts = {}

                def prep(i):
                    s, L, dst, p0, n = wins[i]
                    hT = hTp.next()
                    norm_window(es, pools, XA[s].t.ap()[p0:p0 + n + 2, :], n + 2, hT, [XA[s]])
                    hts[i] = hT

                prep(0)
                for wi, (s, L, dst, p0, n) in enumerate(wins):
                    nw = n + 2
                    hT = hts.pop(wi)
                    st8 = {}

                    def s0(j, hT=hT, nw=nw, st8=st8):
                        wt = wup.next()
                        S.dma("sp", wt[:, :, :].rearrange("p k n -> p (k n)"), WUP.t.ap()[j], reads=[WUP], writes=[wt])
                        pa = psU.next()
                        pb = psU.next()
                        for half, ps in ((0, pa), (1, pb)):
                            for k in range(8):
                                S.op("pe", lambda e, ps=ps, wt=wt, k=k, half=half: e.matmul(
                                    out=ps[:, 0:nw], lhsT=wt[:, k, half * 128:(half + 1) * 128], rhs=hT[:, k, 0:nw], start=(k == 0), stop=(k == 7)),
                                    reads=[wt, hT], writes=[ps])
                        st8[j] = dict(pa=pa, pb=pb)

                    def s1(j, n=n, st8=st8):
                        d_ = st8[j]
                        d_["ca"], d_["cb"] = cva.next(), cvb.next()
                        trip = ((d_["pa"], d_["ca"], j), (d_["pb"], d_["cb"], 32 + j))
                        for ps, c_, ch in trip:
                            S.op("act", lambda e, ps=ps, c_=c_, ch=ch: e.activation(out=c_[:, 0:n], in_=ps[:, 0:n], func=AF.Identity,
                                                                                 scale=cw[:, 0, ch:ch + 1], bias=cw[:, 3, ch:ch + 1]), reads=[ps, cw], writes=[c_])
                        for tp in (1, 2):
                            for ps, c_, ch in trip:
                                S.op("dve", lambda e, ps=ps, c_=c_, ch=ch, tp=tp: e.scalar_tensor_tensor(out=c_[:, 0:n], in0=ps[:, tp:n + tp], scalar=cw[:, tp, ch:ch + 1],
                                                                                                      in1=c_[:, 0:n], op0=ALU.mult, op1=ALU.add), reads=[ps, cw, c_], writes=[c_])

                    def s2(j, n=n, st8=st8):
                        d_ = st8[j]
                        g1 = d_["g1"] = gt.next()
                        ca = d_["ca"]
                        S.op("act", lambda e: e.activation(out=g1[:, 0:n], in_=ca[:, 0:n], func=AF.Square), reads=[ca], writes=[g1])
                        S.op("act", lambda e: e.activation(out=g1[:, 0:n], in_=g1[:, 0:n], func=AF.Identity, scale=0.044715, bias=onec[:]),
                             reads=[g1, onec], writes=[g1])
                        S.op("dve", lambda e: e.tensor_tensor(out=g1[:, 0:n], in0=g1[:, 0:n], in1=ca[:, 0:n], op=ALU.mult), reads=[g1, ca], writes=[g1])

                    def s3(j, n=n, st8=st8):
                        d_ = st8.pop(j)
                        g1, ca, cb = d_["g1"], d_["ca"], d_["cb"]
                        S.op("act", lambda e: e.activation(out=g1[:, 0:n], in_=g1[:, 0:n], func=AF.Sigmoid, scale=1.5957691216057308), reads=[g1], writes=[g1])
                        S.op("pool", lambda e: e.tensor_tensor(out=cb[:, 0:n], in0=ca[:, 0:n], in1=cb[:, 0:n], op=ALU.mult), reads=[ca, cb], writes=[cb])
                        S.op("pool", lambda e: e.tensor_tensor(out=G[:, j, 0:n], in0=g1[:, 0:n], in1=cb[:, 0:n], op=ALU.mult), reads=[g1, cb], writes=[G])

                    stgs = (s0, s1, s2, s3)
                    for t_ in range(32 + 3):
                        for k_ in (3, 2, 1, 0):
                            if 0 <= t_ - k_ < 32:
                                stgs[k_](t_ - k_)
                    if wi + 1 < len(wins):
                        prep(wi + 1)
                    for jj in range((n + 127) // 128):
                        r = min(128, n - jj * 128)
                        ps = psO.next()
                        for nb in range(2):
                            for k in range(32):
                                S.op("pe", lambda e, ps=ps, k=k, nb=nb, jj=jj, r=r: e.matmul(
                                    out=ps[0:r, nb * 512:(nb + 1) * 512], lhsT=G[:, k, jj * 128:jj * 128 + r], rhs=WDN[:, k, nb * 512:(nb + 1) * 512],
                                    start=(k == 0), stop=(k == 31)), reads=[G, WDN], writes=[ps])
                        tok = p0 + jj * 128
                        residual_update(S, pools, tmpf, ps, r, GPF, XA[s].t.ap()[1 + tok:1 + tok + r, :], XA[s],
                                        dst.t.ap()[tok:tok + r, :], dst, epsc)
                S.emit()
    global LAST_NINST
    LAST_NINST = S.ninst
    return nc, list(ins.keys())


def make_in_map(inputs, names, r, LP, LS, depth):
    consts = {"p": hy_consts(LP), "s": hy_consts(LS)}
    m = {}
    for n in names:
        if n.startswith("hc_"):
            _, s, k = n.split("_", 2)
            a = consts[s][k]
        else:
            a = np.asarray(inputs[n], dtype=np.float32)
            if n == "x_prompt":
                a = a[0, :LP]
            elif n == "x_sample":
                a = a[r, :LS]
            elif n == "hgrn_lower_bounds":
                a = a
            else:
                a = a[:depth]
        m[n] = np.ascontiguousarray(a, dtype=np.float32)
    return m


def residual_update(S, pools, tmpf, ps, r, GP, xsrc_ap, xsrc_buf, xdst_ap, xdst_buf, epsc):
    junk = pools["junk"].next()
    st = pools["st"].next()
    S.op("act", lambda e: e.activation(out=junk[0:r, :], in_=ps[0:r, :], func=AF.Square, accum_out=st[0:r, 0:1]), reads=[ps], writes=[junk, st])
    S.op("act", lambda e: e.activation(out=st[0:r, 1:2], in_=st[0:r, 0:1], func=AF.Sqrt, scale=1.0 / D, bias=epsc[0:r, :]), reads=[st, epsc], writes=[st])
    S.op("dve", lambda e: e.reciprocal(out=st[0:r, 2:3], in_=st[0:r, 1:2]), reads=[st], writes=[st])
    t = tmpf.next()
    xt = pools["xt"].next()
    S.dma("sp", xt[0:r, :], xsrc_ap, reads=[xsrc_buf], writes=[xt])
    S.op("dve", lambda e: e.scalar_tensor_tensor(out=t[0:r, :], in0=ps[0:r, :], scalar=st[0:r, 2:3], in1=GP[0:r, :], op0=ALU.mult, op1=ALU.mult),
         reads=[ps, st, GP], writes=[t])
    S.op("pool", lambda e: e.tensor_tensor(out=t[0:r, :], in0=t[0:r, :], in1=xt[0:r, :], op=ALU.add), reads=[t, xt], writes=[t])
    S.dma("pool", xdst_ap, t[0:r, :], reads=[t], writes=[xdst_buf])


def hgrn_phase(nc, S, l, depth, seqs, ins, PROJG, MIX, OFD, ident, epsc):
    with ExitStack() as es:
        RM = sb(nc, es, "hg_rm", [128, 512], F32)
        S.op("pool", lambda e: e.memset(RM[:], 1.0), writes=[RM])
        S.op("pool", lambda e: e.memset(RM[:, 0:512:64], 0.0), reads=[RM], writes=[RM])
        MASKf = sb(nc, es, "hg_maskf", [128, 128], F32)
        S.op("pool", lambda e: e.memset(MASKf[:], 1.0), writes=[MASKf])
        S.op("pool", lambda e: e.affine_select(out=MASKf[:], in_=MASKf[:], pattern=[[1, 128]], compare_op=ALU.is_ge, fill=0.0, base=0,
                                               channel_multiplier=-1), reads=[MASKf], writes=[MASKf])
        S.op("pool", lambda e: e.memset(MASKf[0:64, 64:128], 0.0), reads=[MASKf], writes=[MASKf])
        ONES = sb(nc, es, "hg_ones", [128, 128], BF16)
        S.op("pool", lambda e: e.memset(ONES[:], 1.0), writes=[ONES])
        tb = sb(nc, es, "hg_tb", [128, 2, 4, DEPTH], F32)
        for d_ in range(2):
            for h in range(4):
                S.dma("sp", tb[:, d_, h, :], ins["hgrn_lower_bounds"].t.ap()[d_, :, h * 128:(h + 1) * 128].rearrange("l p -> p l"),
                      reads=[ins["hgrn_lower_bounds"]], writes=[tb], allow_slow_non_contiguous=True)
        te = sb(nc, es, "hg_te", [128, 8, DEPTH], F32)
        S.op("act", lambda e: e.activation(out=te[:], in_=tb[:].rearrange("p a b c -> p (a b) c"), func=AF.Exp), reads=[tb], writes=[te])
        tsum = sb(nc, es, "hg_ts", [128, 8, 3], F32)
        S.op("dve", lambda e: e.tensor_reduce(out=tsum[:, :, 0], in_=te[:], axis=AX.X, op=ALU.add), reads=[te], writes=[tsum])
        if l >= 1:
            S.op("dve", lambda e: e.tensor_reduce(out=tsum[:, :, 1], in_=te[:, :, 1:l + 1], axis=AX.X, op=ALU.add), reads=[te], writes=[tsum])
        else:
            S.op("dve", lambda e: e.memset(tsum[:, :, 1], 0.0), reads=[tsum], writes=[tsum])
        S.op("dve", lambda e: e.reciprocal(out=tsum[:, :, 2], in_=tsum[:, :, 0]), reads=[tsum], writes=[tsum])
        LB = sb(nc, es, "hg_lb", [128, 8, 2], F32)
        S.op("dve", lambda e: e.tensor_tensor(out=LB[:, :, 0], in0=tsum[:, :, 1], in1=tsum[:, :, 2], op=ALU.mult), reads=[tsum], writes=[LB])
        S.op("dve", lambda e: e.tensor_scalar(out=LB[:, :, 1], in0=LB[:, :, 0], scalar1=-1.0, scalar2=1.0, op0=ALU.mult, op1=ALU.add),
             reads=[LB], writes=[LB])

        NH = 4
        ld = {n: Pool(nc, es, "hg_ld" + n, NH, [128, 512], F32) for n in ("f", "q", "i")}
        ldg = Pool(nc, es, "hg_ldg", NH, [128, 512], F32)
        ldo = Pool(nc, es, "hg_ldo", NH, [128, 512], F32)
        w32 = {n: Pool(nc, es, "hg_" + n, NH, [128, 512], F32) for n in ("u", "gl", "kk", "E1", "E2", "E3", "X1", "X2", "X2n", "X3")}
        wbf = {n: Pool(nc, es, "hg_" + n, NH, [128, 512], BF16) for n in ("qh", "qt", "kt", "kh", "vb")}
        OFp = Pool(nc, es, "hg_of", NH, [128, 512], F32)
        SQp = Pool(nc, es, "hg_sq", 2, [128, 512], BF16)
        OBp = Pool(nc, es, "hg_ob", 2, [128, 512], BF16)
        khTp = Pool(nc, es, "hg_khT", NH + 1, [128, 128], BF16)
        vTp = Pool(nc, es, "hg_vT", NH + 1, [128, 128], BF16)
        scmp = Pool(nc, es, "hg_scm", NH + 1, [128, 128], BF16)
        Sbfp = Pool(nc, es, "hg_sbf", 3 * NH, [128, 128], BF16)
        S32s = [sb(nc, es, f"hg_s32_{h}", [128, 128], F32) for h in range(NH)]
        ptr = Pool(nc, es, "hg_ptr", 2, [128, 128], BF16, space="psum")
        psc = Pool(nc, es, "hg_psc", 1, [128, 128], F32, space="psum")
        pkv = Pool(nc, es, "hg_pkv", 2, [128, 128], F32, space="psum")
        po = Pool(nc, es, "hg_po", 2, [128, 128], F32, space="psum")
        pss = Pool(nc, es, "hg_pss", 1, [128, 512], F32, space="psum")

        def V(ap, rev):
            return ap[:, ::-1] if rev else ap

        def bc(t, off):
            a = t[:]
            return bass.AP(a.tensor, a.offset + off, [[a.ap[0][0], 128], [64, 8], [0, 64]])

        def v3(t):
            return t[:].rearrange("p (c j) -> p c j", j=64)

        for s, L in seqs:
            nst = L // 512
            for dr in range(2):
                rev = dr == 1
                for h in range(NH):
                    S.op("pool", lambda e, h=h: e.memset(S32s[h][:], 0.0), reads=[S32s[h]], writes=[S32s[h]])
                order = range(nst - 1, -1, -1) if rev else range(nst)
                for st in order:
                    t0 = st * 512
                    hs = []
                    for h in range(NH):
                        fT, qT, iT = ld["f"].next(), ld["q"].next(), ld["i"].next()
                        S.dma("sp", fT[:], PROJG[s].t.ap()[512 + dr * 512 + h * 128:512 + dr * 512 + (h + 1) * 128, t0:t0 + 512], reads=[PROJG[s]], writes=[fT])
                        S.dma("sp", qT[:], PROJG[s].t.ap()[h * 128:(h + 1) * 128, t0:t0 + 512], reads=[PROJG[s]], writes=[qT])
                        S.dma("sp", iT[:], PROJG[s].t.ap()[1536 + h * 128:1536 + (h + 1) * 128, t0:t0 + 512], reads=[PROJG[s]], writes=[iT])
                        H = dict(rev=rev, li=dr * 4 + h, fT=fT, qT=qT, iT=iT, S32=S32s[h], OF=OFp.next(), OFL=None, gT=None)
                        for n_ in ("u", "gl", "kk", "E1", "E2", "E3", "X1", "X2", "X2n", "X3"):
                            H[n_] = w32[n_].next()
                        for n_ in ("qh", "qt", "kt", "kh", "vb"):
                            H[n_] = wbf[n_].next()
                        H["qs"] = H["u"]
                        if rev:
                            H["OFL"] = ldo.next()
                            H["gT"] = ldg.next()
                            S.dma("sp", H["OFL"][:], OFD[s].t.ap()[h * 128:(h + 1) * 128, t0:t0 + 512], reads=[OFD[s]], writes=[H["OFL"]])
                            S.dma("sp", H["gT"][:], PROJG[s].t.ap()[2048 + h * 128:2048 + (h + 1) * 128, t0:t0 + 512], reads=[PROJG[s]], writes=[H["gT"]])
                        hs.append(H)
                    stages = [
                        lambda H: S.op("act", lambda e: e.activation(out=H["u"][:], in_=H["fT"][:], func=AF.Sigmoid), reads=[H["fT"]], writes=[H["u"]]),
                        lambda H: S.op("act", lambda e: e.activation(out=H["u"][:], in_=H["u"][:], func=AF.Identity, scale=LB[:, H["li"], 1:2],
                                                                     bias=LB[:, H["li"], 0:1]), reads=[H["u"], LB], writes=[H["u"]]),
                        lambda H: (S.op("act", lambda e: e.activation(out=H["gl"][:], in_=H["u"][:], func=AF.Ln), reads=[H["u"]], writes=[H["gl"]]),
                                   S.op("pool", lambda e: e.tensor_scalar(out=H["kk"][:], in0=H["u"][:], scalar1=-1.0, scalar2=1.0, op0=ALU.mult, op1=ALU.add),
                                        reads=[H["u"]], writes=[H["kk"]]),
                                   S.op("pool", lambda e: e.tensor_copy(out=H["vb"][:], in_=V(H["iT"][:], H["rev"])), reads=[H["iT"]], writes=[H["vb"]])),
                        lambda H: (S.op("dve", lambda e: e.tensor_tensor_scan(out=H["E1"][:], data0=RM[:], data1=V(H["gl"][:], H["rev"]), initial=0.0,
                                                                             op0=ALU.mult, op1=ALU.add), reads=[H["gl"], RM], writes=[H["E1"]]),
                                   S.op("act", lambda e: e.activation(out=H["qs"][:], in_=V(H["qT"][:], H["rev"]), func=AF.Silu), reads=[H["qT"], H["kk"], H["gl"]],
                                        writes=[H["qs"]])),
                        lambda H: (S.op("dve", lambda e: e.tensor_tensor(out=v3(H["E2"]), in0=v3(H["E1"]), in1=bc(H["E1"], 31), op=ALU.subtract),
                                        reads=[H["E1"]], writes=[H["E2"]]),
                                   S.op("pool", lambda e: e.tensor_tensor(out=v3(H["E3"]), in0=bc(H["E1"], 63), in1=v3(H["E1"]), op=ALU.subtract),
                                        reads=[H["E1"]], writes=[H["E3"]]),
                                   S.op("act", lambda e: e.activation(out=H["X1"][:], in_=H["E1"][:], func=AF.Exp), reads=[H["E1"]], writes=[H["X1"]])),
                        lambda H: (S.op("act", lambda e: e.activation(out=H["X2"][:], in_=H["E2"][:], func=AF.Exp), reads=[H["E2"]], writes=[H["X2"]]),
                                   S.op("act", lambda e: e.activation(out=H["X2n"][:], in_=H["E2"][:], func=AF.Exp, scale=-1.0), reads=[H["E2"]], writes=[H["X2n"]]),
                                   S.op("act", lambda e: e.activation(out=H["X3"][:], in_=H["E3"][:], func=AF.Exp), reads=[H["E3"]], writes=[H["X3"]]),
                                   S.op("dve", lambda e: e.tensor_tensor(out=H["qh"][:], in0=H["qs"][:], in1=H["X1"][:], op=ALU.mult), reads=[H["qs"], H["X1"]],
                                        writes=[H["qh"]])),
                        lambda H: (S.op("pool", lambda e: e.tensor_tensor(out=H["qt"][:], in0=H["qs"][:], in1=H["X2"][:], op=ALU.mult), reads=[H["qs"], H["X2"]],
                                        writes=[H["qt"]]),
                                   S.op("dve", lambda e: e.tensor_tensor(out=H["kt"][:], in0=V(H["kk"][:], H["rev"]), in1=H["X2n"][:], op=ALU.mult),
                                        reads=[H["kk"], H["X2n"]], writes=[H["kt"]]),
                                   S.op("pool", lambda e: e.tensor_tensor(out=H["kh"][:], in0=V(H["kk"][:], H["rev"]), in1=H["X3"][:], op=ALU.mult),
                                        reads=[H["kk"], H["X3"]], writes=[H["kh"]])),
                    ]
                    for stg_ in stages:
                        for H in hs:
                            stg_(H)
                    for j in range(4):
                        cs = slice(j * 128, (j + 1) * 128)
                        for H in hs:
                            p1 = ptr.next()
                            S.op("pe", lambda e, p1=p1, kh=H["kh"], cs=cs: e.transpose(out=p1[:], in_=kh[:, cs], identity=ident[:]), reads=[H["kh"], ident], writes=[p1])
                            khT = khTp.next()
                            S.op("dve", lambda e, p1=p1, khT=khT: e.tensor_copy(out=khT[:], in_=p1[:]), reads=[p1], writes=[khT])
                            p2 = ptr.next()
                            S.op("pe", lambda e, p2=p2, vb=H["vb"], cs=cs: e.transpose(out=p2[:], in_=vb[:, cs], identity=ident[:]), reads=[H["vb"], ident], writes=[p2])
                            vT = vTp.next()
                            S.op("act", lambda e, p2=p2, vT=vT: e.copy(out=vT[:], in_=p2[:]), reads=[p2], writes=[vT])
                            sc = psc.next()
                            S.op("pe", lambda e, sc=sc, kt=H["kt"], qt=H["qt"], cs=cs: e.matmul(out=sc[:], lhsT=kt[:, cs], rhs=qt[:, cs], start=True, stop=True),
                                 reads=[H["kt"], H["qt"]], writes=[sc])
                            scm = scmp.next()
                            S.op("dve", lambda e, sc=sc, scm=scm: e.tensor_tensor(out=scm[:], in0=sc[:], in1=MASKf[:], op=ALU.mult), reads=[sc, MASKf], writes=[scm])
                            H["khT"], H["vT"], H["scm"] = khT, vT, scm
                        for c in range(2):
                            for H in hs:
                                Sbf = Sbfp.next()
                                H["Sbf%d" % c] = Sbf
                                S32 = H["S32"]
                                S.op("act", lambda e, Sbf=Sbf, S32=S32: e.copy(out=Sbf[:], in_=S32[:]), reads=[S32], writes=[Sbf])
                                ccol = j * 128 + c * 64
                                kv = pkv.next()
                                S.op("pe", lambda e, kv=kv, khT=H["khT"], vT=H["vT"], c=c: e.matmul(out=kv[:], lhsT=khT[c * 64:(c + 1) * 64, :],
                                                                                                   rhs=vT[c * 64:(c + 1) * 64, :], start=True, stop=True),
                                     reads=[H["khT"], H["vT"]], writes=[kv])
                                xb_col = ccol + 63
                                S.op("dve", lambda e, kv=kv, X1=H["X1"], xb_col=xb_col, S32=S32: e.scalar_tensor_tensor(
                                    out=S32[:], in0=S32[:], scalar=X1[:, xb_col:xb_col + 1], in1=kv[:], op0=ALU.mult, op1=ALU.add),
                                    reads=[S32, H["X1"], kv], writes=[S32])
                        for H in hs:
                            o_ps = po.next()
                            S.op("pe", lambda e, o_ps=o_ps, vT=H["vT"], scm=H["scm"]: e.matmul(out=o_ps[:], lhsT=vT[:], rhs=scm[:], start=True, stop=False),
                                 reads=[H["vT"], H["scm"]], writes=[o_ps])
                            for c in range(2):
                                ccol = j * 128 + c * 64
                                Sbf = H["Sbf%d" % c]
                                S.op("pe", lambda e, o_ps=o_ps, Sbf=Sbf, qh=H["qh"], c=c, ccol=ccol: e.matmul(
                                    out=o_ps[:, c * 64:(c + 1) * 64], lhsT=Sbf[:], rhs=qh[:, ccol:ccol + 64], start=False, stop=(c == 1)),
                                    reads=[Sbf, H["qh"]], writes=[o_ps])
                            if not rev:
                                S.op("act", lambda e, OF=H["OF"], o_ps=o_ps, cs=cs: e.copy(out=OF[:, cs], in_=o_ps[:]), reads=[o_ps], writes=[H["OF"]])
                            else:
                                S.op("dve", lambda e, OF=H["OF"], o_ps=o_ps, cs=cs, OFL=H["OFL"]: e.tensor_tensor(
                                    out=OF[:, ::-1][:, cs], in0=o_ps[:], in1=OFL[:, ::-1][:, cs], op=ALU.add), reads=[o_ps, H["OFL"]], writes=[H["OF"]])
                    for h, H in enumerate(hs):
                        OF = H["OF"]
                        if not rev:
                            S.dma("sp", OFD[s].t.ap()[h * 128:(h + 1) * 128, t0:t0 + 512], OF[:], reads=[OF], writes=[OFD[s]])
                        else:
                            SQ, OB, gT = SQp.next(), OBp.next(), H["gT"]
                            RS = H["OFL"]
                            S.op("act", lambda e, SQ=SQ, OF=OF: e.activation(out=SQ[:], in_=OF[:], func=AF.Square), reads=[OF], writes=[SQ])
                            ss = pss.next()
                            S.op("pe", lambda e, ss=ss, SQ=SQ: e.matmul(out=ss[:], lhsT=ONES[:], rhs=SQ[:], start=True, stop=True), reads=[ONES, SQ], writes=[ss])
                            S.op("act", lambda e, ss=ss, RS=RS: e.activation(out=RS[:], in_=ss[:], func=AF.Sqrt, scale=1.0 / 128, bias=epsc[:]),
                                 reads=[ss, epsc], writes=[RS])
                            S.op("dve", lambda e, RS=RS: e.reciprocal(out=RS[:], in_=RS[:]), reads=[RS], writes=[RS])
                            S.op("act", lambda e, gT=gT: e.activation(out=gT[:], in_=gT[:], func=AF.Silu), reads=[gT], writes=[gT])
                            S.op("dve", lambda e, RS=RS, OF=OF: e.tensor_tensor(out=RS[:], in0=RS[:], in1=OF[:], op=ALU.mult), reads=[RS, OF], writes=[RS])
                            S.op("pool", lambda e, RS=RS, gT=gT, OB=OB: e.tensor_tensor(out=OB[:], in0=RS[:], in1=gT[:], op=ALU.mult), reads=[RS, gT], writes=[OB])
                            S.dma("sp", MIX[s].t.ap()[512 + h * 128:512 + (h + 1) * 128, t0:t0 + 512], OB[:], reads=[OB], writes=[MIX[s]])
        S.emit()


LAST_NINST = 0
MAGIC = 12582912.0
TWO_PI = 6.283185


def hy_consts(L):
    B = L // 128
    n1 = np.arange(128)[:, None]
    k1 = np.arange(128)[None, :]
    th = 2 * np.pi * n1 * (k1 + 0.5) / 256
    C = {}
    C["F1CAT"] = np.concatenate([np.cos(th), -np.sin(th)], axis=1)
    n2 = np.arange(B)[:, None]
    ph = np.pi * n2 * (k1 + 0.5) / L
    C["TWR"] = np.cos(ph)
    C["TWI"] = -np.sin(ph)
    k2 = np.arange(B)[None, :]
    a2 = 2 * np.pi * n2 * k2 / B
    C["F2RE"] = np.cos(a2)
    C["F2IM"] = -np.sin(a2)
    C["F2IMN"] = np.sin(a2)
    fr, fi = np.cos(a2), np.sin(a2)
    C["F2I_A"] = np.concatenate([fr, fi], axis=1)
    C["F2I_B"] = np.concatenate([-fi, fr], axis=1)
    phi = np.pi * (np.arange(128)[:, None] + 0.5) * np.arange(B)[None, :] / L
    C["ITWR"] = np.cos(phi)
    C["ITWI"] = np.sin(phi)
    thi = 2 * np.pi * (np.arange(128)[:, None] + 0.5) * np.arange(128)[None, :] / 256
    C["F1I_RE"] = np.cos(thi) / L
    C["F1I_IM"] = -np.sin(thi) / L
    t = np.linspace(0.0, 1.0, L, dtype=np.float32)[:, None]
    ang = (np.float32(2.0 * math.pi / L) * np.arange(L, dtype=np.float32))[:, None]
    bands = np.linspace(1e-4, 15, 16, dtype=np.float32)[None, :]
    feats = np.concatenate([t, np.cos(bands * ang), -np.sin(bands * ang)], axis=-1)
    C["FEAT"] = feats.T
    deltas = np.abs(np.linspace(math.log(1e-2) / 1.5, math.log(1e-2) / 0.3, 512, dtype=np.float32))
    rows = np.arange(1024) % 512
    C["NDL"] = (-deltas[rows] / (L - 1)).reshape(8, 128).T
    C["IOTA"] = np.arange(512, dtype=np.float32)[None, :]
    C["IOTAB"] = (512.0 * np.arange(max(1, L // 512), dtype=np.float32))[None, :]
    return {k: np.ascontiguousarray(v, dtype=np.float32) for k, v in C.items()}


def hyena_inputs(nc, inp, depth, seqs):
    C = {}
    for s, L in seqs:
        for k, v in hy_consts(L).items():
            C[(s, k)] = inp(f"hc_{s}_{k}", list(v.shape))
    for n, sh in [("hyena_conv_w", [depth, 3, 1536]), ("hyena_conv_b", [depth, 1536]), ("filt_w1", [depth, 33, 64]),
                  ("filt_b1", [depth, 64]), ("filt_w2", [depth, 64, 64]), ("filt_b2", [depth, 64]), ("filt_w3", [depth, 64, 64]),
                  ("filt_b3", [depth, 64]), ("filt_w4", [depth, 64, 1024]), ("filt_freq", [depth, 64]), ("hyena_skip", [depth, 512])]:
        inp(n, sh)
    for s, L in seqs:
        C[(s, "HFD")] = dram(nc, "HFD_" + s, [1024, L], BF16)
        C[(s, "ZD")] = dram(nc, "ZD_" + s, [512, L], BF16)
        C[(s, "X0C")] = dram(nc, "X0C_" + s, [512, L], BF16)
    return C


def hyena_phase(nc, S, l, depth, seqs, ins, PROJH, MIX, ident, epsc, C):
    with ExitStack() as es:
        w1 = sb(nc, es, "hy_w1", [33, 64], F32)
        w2 = sb(nc, es, "hy_w2", [64, 64], F32)
        w3 = sb(nc, es, "hy_w3", [64, 64], F32)
        w4f = sb(nc, es, "hy_w4f", [64, 1024], F32)
        w4 = sb(nc, es, "hy_w4", [64, 1024], BF16)
        S.dma("sp", w1[:], ins["filt_w1"].t.ap()[l], reads=[ins["filt_w1"]], writes=[w1])
        S.dma("sp", w2[:], ins["filt_w2"].t.ap()[l], reads=[ins["filt_w2"]], writes=[w2])
        S.dma("sp", w3[:], ins["filt_w3"].t.ap()[l], reads=[ins["filt_w3"]], writes=[w3])
        S.dma("sp", w4f[:], ins["filt_w4"].t.ap()[l], reads=[ins["filt_w4"]], writes=[w4f])
        S.op("act", lambda e: e.copy(out=w4[:], in_=w4f[:]), reads=[w4f], writes=[w4])
        pv = sb(nc, es, "hy_pv", [64, 8], F32)
        S.dma("sp", pv[:, 0:1], ins["filt_freq"].t.ap()[l].rearrange("(p o) -> p o", o=1), reads=[ins["filt_freq"]], writes=[pv])
        for i_, nm in enumerate(("filt_b1", "filt_b2", "filt_b3")):
            S.dma("sp", pv[:, 1 + i_:2 + i_], ins[nm].t.ap()[l].rearrange("(p o) -> p o", o=1), reads=[ins[nm]], writes=[pv])
        S.op("dve", lambda e: e.tensor_scalar(out=pv[:, 4:5], in0=pv[:, 0:1], scalar1=1.0 / (2 * math.pi), scalar2=None, op0=ALU.mult),
             reads=[pv], writes=[pv])
        for i_ in range(3):
            S.op("dve", lambda e, i_=i_: e.tensor_tensor(out=pv[:, 5 + i_:6 + i_], in0=pv[:, 1 + i_:2 + i_], in1=pv[:, 4:5], op=ALU.mult),
                 reads=[pv], writes=[pv])
        skc = sb(nc, es, "hy_skc", [128, 4], F32)
        S.dma("sp", skc[:], ins["hyena_skip"].t.ap()[l].rearrange("(k p) -> p k", p=128), reads=[ins["hyena_skip"]], writes=[skc],
              allow_slow_non_contiguous=True)
        iota = sb(nc, es, "hy_iota", [128, 512], F32)
        fpool = Pool(nc, es, "hy_feat", 2, [33, 512], F32)
        hp = Pool(nc, es, "hy_h", 3, [64, 512], F32)
        up = Pool(nc, es, "hy_u", 2, [64, 512], F32)
        rp = Pool(nc, es, "hy_r", 2, [64, 512], F32)
        pm = Pool(nc, es, "hy_pm", 2, [64, 512], F32, space="psum")
        pf = Pool(nc, es, "hy_pf", 2, [128, 512], F32, space="psum")
        hfo = Pool(nc, es, "hy_hfo", 3, [128, 512], BF16)
        winj = sb(nc, es, "hy_winj", [128, 512], F32)
        for s, L in seqs:
            nblk = max(1, L // 512)
            bw = min(512, L)
            H3 = sb(nc, es, "hy_H3" + s, [64, L], BF16)
            ndl = sb(nc, es, "hy_ndl" + s, [128, 8], F32)
            iotab = sb(nc, es, "hy_iotab" + s, [128, nblk], F32)
            wblk = sb(nc, es, "hy_wblk" + s, [128, nblk], F32)
            S.dma("sp", ndl[:], C[(s, "NDL")].t.ap(), reads=[C[(s, "NDL")]], writes=[ndl])
            S.dma("sp", iota[:], bass.AP(C[(s, "IOTA")].t, 0, [[0, 128], [1, 512]]), reads=[C[(s, "IOTA")]], writes=[iota])
            S.dma("sp", iotab[:], bass.AP(C[(s, "IOTAB")].t, 0, [[0, 128], [1, nblk]]), reads=[C[(s, "IOTAB")]], writes=[iotab])
            for b in range(nblk):
                ft = fpool.next()
                S.dma("sp", ft[:, 0:bw], C[(s, "FEAT")].t.ap()[:, b * 512:b * 512 + bw], reads=[C[(s, "FEAT")]], writes=[ft])
                cur = ft
                for li, (w_, kdim) in enumerate(((w1, 33), (w2, 64), (w3, 64))):
                    ps = pm.next()
                    S.op("pe", lambda e, ps=ps, w_=w_, cur=cur, kdim=kdim, bw=bw: e.matmul(out=ps[:, 0:bw], lhsT=w_[0:kdim, :], rhs=cur[0:kdim, 0:bw],
                                                                                 start=True, stop=True), reads=[w_, cur], writes=[ps])
                    u = up.next()
                    r_ = rp.next()
                    S.op("dve", lambda e, ps=ps, u=u, li=li, bw=bw: e.tensor_scalar(out=u[:, 0:bw], in0=ps[:, 0:bw], scalar1=pv[:, 4:5], scalar2=pv[:, 5 + li:6 + li],
                                                                          op0=ALU.mult, op1=ALU.add), reads=[ps, pv], writes=[u])
                    S.op("dve", lambda e, u=u, r_=r_, bw=bw: e.tensor_scalar(out=r_[:, 0:bw], in0=u[:, 0:bw], scalar1=MAGIC, scalar2=MAGIC, op0=ALU.add, op1=ALU.subtract),
                         reads=[u], writes=[r_])
                    S.op("dve", lambda e, u=u, r_=r_, bw=bw: e.tensor_tensor(out=u[:, 0:bw], in0=u[:, 0:bw], in1=r_[:, 0:bw], op=ALU.subtract), reads=[u, r_], writes=[u])
                    if li < 2:
                        hh = hp.next()
                        S.op("act", lambda e, u=u, hh=hh, bw=bw: e.activation(out=hh[:, 0:bw], in_=u[:, 0:bw], func=AF.Sin, scale=TWO_PI), reads=[u], writes=[hh])
                        cur = hh
                    else:
                        S.op("act", lambda e, u=u, b=b, bw=bw, H3=H3: e.activation(out=H3[:, b * 512:b * 512 + bw], in_=u[:, 0:bw], func=AF.Sin, scale=TWO_PI),
                             reads=[u], writes=[H3])
            for rb in range(8):
                S.op("act", lambda e, rb=rb, ndl=ndl: e.activation(out=winj[:], in_=iota[:], func=AF.Exp, scale=ndl[:, rb:rb + 1]), reads=[iota, ndl], writes=[winj])
                S.op("act", lambda e, rb=rb, ndl=ndl, wblk=wblk, iotab=iotab: e.activation(out=wblk[:], in_=iotab[:], func=AF.Exp, scale=ndl[:, rb:rb + 1]), reads=[iotab, ndl], writes=[wblk])
                for b in range(nblk):
                    ps = pf.next()
                    S.op("pe", lambda e, ps=ps, rb=rb, b=b, bw=bw, H3=H3: e.matmul(out=ps[:, 0:bw], lhsT=w4[:, rb * 128:(rb + 1) * 128], rhs=H3[:, b * 512:b * 512 + bw],
                                                                  start=True, stop=True), reads=[w4, H3], writes=[ps])
                    o = hfo.next()
                    S.op("dve", lambda e, ps=ps, o=o, b=b, bw=bw, wblk=wblk: e.scalar_tensor_tensor(out=o[:, 0:bw], in0=ps[:, 0:bw], scalar=wblk[:, b:b + 1], in1=winj[:, 0:bw],
                                                                               op0=ALU.mult, op1=ALU.mult), reads=[ps, wblk, winj], writes=[o])
                    if b == 0:
                        if rb < 4:
                            S.op("dve", lambda e, o=o, rb=rb: e.tensor_tensor(out=o[:, 0:1], in0=o[:, 0:1], in1=skc[:, rb:rb + 1], op=ALU.add),
                                 reads=[o, skc], writes=[o])
                        else:
                            S.op("dve", lambda e, o=o: e.memset(o[:, 0:1], 0.0), reads=[o], writes=[o])
                    S.dma("pool", C[(s, "HFD")].t.ap()[rb * 128:(rb + 1) * 128, b * 512:b * 512 + bw], o[:, 0:bw], reads=[o], writes=[C[(s, "HFD")]])
        S.emit()

    with ExitStack() as es:
        hcw = sb(nc, es, "hy_cw", [128, 4, 12], F32)
        for tp in range(3):
            S.dma("sp", hcw[:, tp, :], ins["hyena_conv_w"].t.ap()[l, tp].rearrange("(k p) -> p k", p=128), reads=[ins["hyena_conv_w"]], writes=[hcw],
                  allow_slow_non_contiguous=True)
        S.dma("sp", hcw[:, 3, :], ins["hyena_conv_b"].t.ap()[l].rearrange("(k p) -> p k", p=128), reads=[ins["hyena_conv_b"]], writes=[hcw],
              allow_slow_non_contiguous=True)
        NPmax = 2048
        xin_p = Pool(nc, es, "hy_xin", 6, [128, NPmax + 2], BF16)
        acc_p = Pool(nc, es, "hy_acc", 6, [128, NPmax], F32)
        ob_p = Pool(nc, es, "hy_ob", 4, [128, NPmax], BF16)
        for s, L in seqs:
            NP = min(NPmax, L)
            for cb in range(4):
                for c0 in range(0, L, NP):
                    accs = []
                    xas = []
                    for a in range(3):
                        xa = xin_p.next()
                        row0 = a * 512 + cb * 128
                        S.dma("sp", xa[:, 0:NP + 2], PROJH[s].t.ap()[row0:row0 + 128, c0:c0 + NP + 2], reads=[PROJH[s]], writes=[xa])
                        xas.append(xa)
                        accs.append(acc_p.next())
                    for a in range(3):
                        kcol = a * 4 + cb
                        S.op("act", lambda e, xa=xas[a], acc=accs[a], kcol=kcol, NP=NP: e.activation(out=acc[:, 0:NP], in_=xa[:, 0:NP], func=AF.Identity,
                                                                                              scale=hcw[:, 0, kcol:kcol + 1], bias=hcw[:, 3, kcol:kcol + 1]),
                             reads=[xas[a], hcw], writes=[accs[a]])
                    for tp in (1, 2):
                        for a in range(3):
                            kcol = a * 4 + cb
                            S.op("dve", lambda e, xa=xas[a], acc=accs[a], kcol=kcol, tp=tp, NP=NP: e.scalar_tensor_tensor(
                                out=acc[:, 0:NP], in0=xa[:, tp:tp + NP], scalar=hcw[:, tp, kcol:kcol + 1], in1=acc[:, 0:NP], op0=ALU.mult, op1=ALU.add),
                                reads=[xas[a], hcw, accs[a]], writes=[accs[a]])
                    o0 = ob_p.next()
                    S.op("act", lambda e, o0=o0, a0=accs[0], NP=NP: e.copy(out=o0[:, 0:NP], in_=a0[:, 0:NP]), reads=[accs[0]], writes=[o0])
                    S.dma("pool", C[(s, "X0C")].t.ap()[cb * 128:(cb + 1) * 128, c0:c0 + NP], o0[:, 0:NP], reads=[o0], writes=[C[(s, "X0C")]])
                    oz = ob_p.next()
                    S.op("pool", lambda e, oz=oz, a1=accs[1], a2=accs[2], NP=NP: e.tensor_tensor(out=oz[:, 0:NP], in0=a1[:, 0:NP], in1=a2[:, 0:NP], op=ALU.mult),
                         reads=[accs[1], accs[2]], writes=[oz])
                    S.dma("pool", C[(s, "ZD")].t.ap()[cb * 128:(cb + 1) * 128, c0:c0 + NP], oz[:, 0:NP], reads=[oz], writes=[C[(s, "ZD")]])
        S.emit()

    for s, L in seqs:
        B = L // 128
        with ExitStack() as es:
            def cload(name, shape, dt):
                stg = sb(nc, es, "hy_cs_" + name, shape, F32)
                S.dma("sp", stg[:], C[(s, name)].t.ap(), reads=[C[(s, name)]], writes=[stg])
                if dt == F32:
                    return stg
                t = sb(nc, es, "hy_c_" + name, shape, BF16)
                S.op("act", lambda e: e.copy(out=t[:], in_=stg[:]), reads=[stg], writes=[t])
                return t
            F1CAT = cload("F1CAT", [128, 256], BF16)
            TWR = cload("TWR", [B, 128], F32)
            TWI = cload("TWI", [B, 128], F32)
            F2RE = cload("F2RE", [B, B], BF16)
            F2IM = cload("F2IM", [B, B], BF16)
            F2IMN = cload("F2IMN", [B, B], BF16)
            F2I_A = cload("F2I_A", [B, 2 * B], BF16)
            F2I_B = cload("F2I_B", [B, 2 * B], BF16)
            ITWR = cload("ITWR", [128, B], F32)
            ITWI = cload("ITWI", [128, B], F32)
            F1I_RE = cload("F1I_RE", [128, 128], BF16)
            F1I_IM = cload("F1I_IM", [128, 128], BF16)
            ZG = sb(nc, es, "hy_zg", [128, 64 * B], BF16)
            ATR = sb(nc, es, "hy_atr", [128, 64 * 128], BF16)
            ATI = sb(nc, es, "hy_ati", [128, 64 * 128], BF16)
            GR = sb(nc, es, "hy_gr", [128, 64 * 128], BF16)
            GI = sb(nc, es, "hy_gi", [128, 64 * 128], BF16)
            PR = sb(nc, es, "hy_pr", [128, 64 * 128], BF16)
            PI = sb(nc, es, "hy_pi", [128, 64 * 128], BF16)
            X0G = sb(nc, es, "hy_x0g", [128, 64 * B], BF16)
            U = sb(nc, es, "hy_uu", [128, 64 * B], BF16)
            TS = [[sb(nc, es, f"hy_t{q}{i}", [128, 512], F32) for i in range(4)] for q in range(2)]
            SSt = sb(nc, es, "hy_ss", [128, 2, B], F32)
            PS4s = [Buf(es.enter_context(nc.psum_tensor(_uname("hy_ps4"), [128, 1024], F32)), "ps4") for _ in range(2)]
            PSRs = [Buf(es.enter_context(nc.psum_tensor(_uname("hy_psr"), [128, 512], F32)), "psr") for _ in range(2)]
            PSIs = [Buf(es.enter_context(nc.psum_tensor(_uname("hy_psi"), [128, 512], F32)), "psi") for _ in range(2)]
            tctr = [0]

            def bcast_mid(t, np_, mid, inner):
                a = t[0:np_, 0:inner]
                return bass.AP(a.tensor, a.offset, [[a.ap[0][0], np_], [0, mid], [1, inner]])

            def cmul_evac(np_, re, im, cr, ci, out_re, out_im, mid, inner, rd, wr):
                T = TS[tctr[0] % 2]
                tctr[0] += 1
                n = mid * inner
                tv = [t[0:np_, 0:n].rearrange("p (c k) -> p c k", c=mid) for t in T]
                S.op("dve", lambda e: e.tensor_tensor(out=tv[0], in0=re, in1=cr, op=ALU.mult), reads=rd, writes=[T[0]])
                S.op("dve", lambda e: e.tensor_tensor(out=tv[1], in0=im, in1=ci, op=ALU.mult), reads=rd, writes=[T[1]])
                S.op("dve", lambda e: e.tensor_tensor(out=tv[2], in0=re, in1=ci, op=ALU.mult), reads=rd, writes=[T[2]])
                S.op("dve", lambda e: e.tensor_tensor(out=tv[3], in0=im, in1=cr, op=ALU.mult), reads=rd, writes=[T[3]])
                S.op("pool", lambda e: e.tensor_tensor(out=out_re, in0=tv[0], in1=tv[1], op=ALU.subtract), reads=[T[0], T[1]], writes=wr)
                S.op("pool", lambda e: e.tensor_tensor(out=out_im, in0=tv[2], in1=tv[3], op=ALU.add), reads=[T[2], T[3]], writes=wr)

            def fwd_fft(src_rows_ap, src_buf, mode):
                S.dma("sp", ZG[:, :].rearrange("p (c n) -> p c n", c=64), src_rows_ap.rearrange("c (n1 n2) -> n1 c n2", n2=B), reads=[src_buf], writes=[ZG])
                atr = ATR[0:B, :].rearrange("p (c k) -> p c k", c=64)
                ati = ATI[0:B, :].rearrange("p (c k) -> p c k", c=64)
                for cb in range(16):
                    PS4 = PS4s[cb % 2]
                    for ci in range(4):
                        c = cb * 4 + ci
                        S.op("pe", lambda e, c=c, ci=ci, PS4=PS4: e.matmul(out=PS4[0:B, ci * 256:(ci + 1) * 256], lhsT=ZG[:, c * B:(c + 1) * B], rhs=F1CAT[:],
                                                                          start=True, stop=True), reads=[ZG, F1CAT], writes=[PS4])
                    pv4 = PS4[0:B, :].rearrange("p (c t k) -> p c t k", c=4, t=2)
                    cmul_evac(B, pv4[:, :, 0, :], pv4[:, :, 1, :], bcast_mid(TWR, B, 4, 128), bcast_mid(TWI, B, 4, 128),
                              atr[:, cb * 4:(cb + 1) * 4, :], ati[:, cb * 4:(cb + 1) * 4, :], 4, 128, [PS4, TWR, TWI], [ATR, ATI])
                for cb in range(16):
                    PSR, PSI = PSRs[cb % 2], PSIs[cb % 2]
                    cols = slice(cb * 512, (cb + 1) * 512)
                    S.op("pe", lambda e, cols=cols, PSR=PSR: e.matmul(out=PSR[0:B, :], lhsT=F2RE[:], rhs=ATR[0:B, cols], start=True, stop=False),
                         reads=[F2RE, ATR], writes=[PSR])
                    S.op("pe", lambda e, cols=cols, PSR=PSR: e.matmul(out=PSR[0:B, :], lhsT=F2IMN[:], rhs=ATI[0:B, cols], start=False, stop=True),
                         reads=[F2IMN, ATI], writes=[PSR])
                    S.op("pe", lambda e, cols=cols, PSI=PSI: e.matmul(out=PSI[0:B, :], lhsT=F2IM[:], rhs=ATR[0:B, cols], start=True, stop=False),
                         reads=[F2IM, ATR], writes=[PSI])
                    S.op("pe", lambda e, cols=cols, PSI=PSI: e.matmul(out=PSI[0:B, :], lhsT=F2RE[:], rhs=ATI[0:B, cols], start=False, stop=True),
                         reads=[F2RE, ATI], writes=[PSI])
                    gs = cols
                    if mode == "set":
                        S.op("act", lambda e, gs=gs, PSR=PSR: e.copy(out=GR[0:B, gs], in_=PSR[0:B, :]), reads=[PSR], writes=[GR])
                        S.op("act", lambda e, gs=gs, PSI=PSI: e.copy(out=GI[0:B, gs], in_=PSI[0:B, :]), reads=[PSI], writes=[GI])
                    elif mode == "accconj":
                        S.op("dve", lambda e, gs=gs, PSR=PSR: e.tensor_tensor(out=GR[0:B, gs], in0=PSR[0:B, :], in1=GR[0:B, gs], op=ALU.add), reads=[PSR, GR], writes=[GR])
                        S.op("dve", lambda e, gs=gs, PSI=PSI: e.tensor_tensor(out=GI[0:B, gs], in0=GI[0:B, gs], in1=PSI[0:B, :], op=ALU.subtract), reads=[PSI, GI], writes=[GI])
                    else:
                        v = lambda t: t[0:B, gs].rearrange("p (c k) -> p c k", c=4)
                        vp = lambda t: t[0:B, :].rearrange("p (c k) -> p c k", c=4)
                        cmul_evac(B, vp(PSR), vp(PSI), v(GR), v(GI), v(PR), v(PI), 4, 128, [PSR, PSI, GR, GI], [PR, PI])

            for g in range(8):
                fwd_fft(C[(s, "HFD")].t.ap()[g * 64:(g + 1) * 64, :], C[(s, "HFD")], "set")
                fwd_fft(C[(s, "HFD")].t.ap()[512 + g * 64:512 + (g + 1) * 64, :], C[(s, "HFD")], "accconj")
                S.dma("pool", X0G[:, :].rearrange("p (c n) -> p c n", c=64), C[(s, "X0C")].t.ap()[g * 64:(g + 1) * 64, :].rearrange("c (n1 n2) -> n1 c n2", n2=B),
                      reads=[C[(s, "X0C")]], writes=[X0G])
                fwd_fft(C[(s, "ZD")].t.ap()[g * 64:(g + 1) * 64, :], C[(s, "ZD")], "mul")
                nb2 = 1024 // (2 * B)
                for cb in range(64 // nb2):
                    PS4 = PS4s[cb % 2]
                    for ci in range(nb2):
                        c = cb * nb2 + ci
                        S.op("pe", lambda e, c=c, ci=ci, PS4=PS4: e.matmul(out=PS4[:, ci * 2 * B:(ci + 1) * 2 * B], lhsT=PR[0:B, c * 128:(c + 1) * 128], rhs=F2I_A[:],
                                                                          start=True, stop=False), reads=[PR, F2I_A], writes=[PS4])
                        S.op("pe", lambda e, c=c, ci=ci, PS4=PS4: e.matmul(out=PS4[:, ci * 2 * B:(ci + 1) * 2 * B], lhsT=PI[0:B, c * 128:(c + 1) * 128], rhs=F2I_B[:],
                                                                          start=False, stop=True), reads=[PI, F2I_B], writes=[PS4])
                    pv4 = PS4[:, :].rearrange("p (c t n) -> p c t n", c=nb2, t=2)
                    dtr = ATR[:, 0:64 * B].rearrange("p (c n) -> p c n", c=64)
                    dti = ATI[:, 0:64 * B].rearrange("p (c n) -> p c n", c=64)
                    cmul_evac(128, pv4[:, :, 0, :], pv4[:, :, 1, :], bcast_mid(ITWR, 128, nb2, B), bcast_mid(ITWI, 128, nb2, B),
                              dtr[:, cb * nb2:(cb + 1) * nb2, :], dti[:, cb * nb2:(cb + 1) * nb2, :], nb2, B, [PS4, ITWR, ITWI], [ATR, ATI])
                ncol = 64 * B
                pxs = [PSRs[0], PSIs[0], PSRs[1], PSIs[1]]
                for qi, q0 in enumerate(range(0, ncol, 512)):
                    PX = pxs[qi % 4]
                    cols = slice(q0, q0 + 512)
                    S.op("pe", lambda e, PX=PX, cols=cols: e.matmul(out=PX[:, :], lhsT=F1I_RE[:], rhs=ATR[:, cols], start=True, stop=False),
                         reads=[F1I_RE, ATR], writes=[PX])
                    S.op("pe", lambda e, PX=PX, cols=cols: e.matmul(out=PX[:, :], lhsT=F1I_IM[:], rhs=ATI[:, cols], start=False, stop=True),
                         reads=[F1I_IM, ATI], writes=[PX])
                    S.op("dve", lambda e, PX=PX, cols=cols: e.tensor_tensor(out=U[:, cols], in0=PX[:, :], in1=X0G[:, cols], op=ALU.mult),
                         reads=[PX, X0G], writes=[U])
                S.op("act", lambda e: e.activation(out=PR[:, 0:ncol], in_=U[:, 0:ncol], func=AF.Square), reads=[U], writes=[PR])
                S.op("dve", lambda e: e.tensor_reduce(out=SSt[:, 0, :], in_=PR[:, 0:ncol].rearrange("p (c n) -> p n c", c=64), axis=AX.X, op=ALU.add),
                     reads=[PR], writes=[SSt])
                S.op("act", lambda e: e.activation(out=SSt[:, 1, :], in_=SSt[:, 0, :], func=AF.Sqrt, scale=1.0 / 64, bias=epsc[:]), reads=[SSt, epsc], writes=[SSt])
                S.op("dve", lambda e: e.reciprocal(out=SSt[:, 1, :], in_=SSt[:, 1, :]), reads=[SSt], writes=[SSt])
                rsb = bass.AP(SSt.t, SSt[:, 1, :].offset, [[SSt[:].ap[0][0], 128], [0, 64], [1, B]])
                S.op("dve", lambda e, rsb=rsb: e.tensor_tensor(out=ZG[:, :].rearrange("p (c n) -> p c n", c=64), in0=U[:, :].rearrange("p (c n) -> p c n", c=64),
                                                            in1=rsb, op=ALU.mult), reads=[U, SSt], writes=[ZG])
                S.dma("pool", MIX[s].t.ap()[g * 64:(g + 1) * 64, :].rearrange("c (n1 n2) -> n1 c n2", n2=B), ZG[:, :].rearrange("p (c n) -> p c n", c=64),
                      reads=[ZG], writes=[MIX[s]])
            S.emit()


MIXERS = True


def kernel(**inputs):
    from concourse.bass_utils import run_bass_kernel_spmd
    nc, names = build(LP_FULL, LS_FULL, DEPTH, mixers=MIXERS)
    in_maps = [make_in_map(inputs, names, r, LP_FULL, LS_FULL, DEPTH) for r in range(4)]
    res = run_bass_kernel_spmd(nc, in_maps, core_ids=list(range(4)))
    y_prompt = np.asarray(res.results[0]["y_prompt"], dtype=np.float32)[None]
    y_sample = np.stack([np.asarray(res.results[r]["y_sample"], dtype=np.float32) for r in range(4)], axis=0)
    return (y_prompt, y_sample)
```
